# Optimizing a Trainium2 kernel written in Bass

```python
import jax, jax.numpy as jnp
from jax import lax
import numpy as np

D_MODEL = 2048
BATCH = 2
SEQ = 8192
DEPTH = 2
DEC_BATCH = 1
DEC_SEQ = 16384
PAST_LEN = 128

N_MEM = 256
D_MIX = D_MODEL
HEAD_DIM = 64
ATTN_WIDTH = D_MIX // 2
N_Q_HEADS = ATTN_WIDTH // HEAD_DIM
N_KV_HEADS = 4
KV_GROUP = N_Q_HEADS // N_KV_HEADS
KV_WIDTH = N_KV_HEADS * HEAD_DIM
WINDOW = 128
BLOCK = 128
CONV_WIDTH = D_MIX // 4
CONV_K = 3
N_X_HEADS = 4
X_WIDTH = D_MIX // 4
X_HEAD_DIM = X_WIDTH // N_X_HEADS
ROPE_THETA = 10000.0
EPS = 1e-6
IN_SIZES = (ATTN_WIDTH, KV_WIDTH, KV_WIDTH, ATTN_WIDTH,
            CONV_WIDTH, CONV_WIDTH, CONV_WIDTH, CONV_WIDTH,
            X_WIDTH, X_WIDTH)
D_IN = sum(IN_SIZES)

kernel_name = "hymba_style_window_gqa_shortconv_memxattn_encoder"


def rmsnorm(x, g):
    x32 = x.astype(jnp.float32)
    y = x32 * lax.rsqrt(jnp.mean(x32 * x32, axis=-1, keepdims=True) + EPS)
    return y.astype(x.dtype) * g


def split_cols(p):
    outs, start = [], 0
    for size in IN_SIZES:
        outs.append(p[..., start:start + size])
        start += size
    return outs


def rope(x):
    s, d = x.shape[1], x.shape[-1]
    inv_freq = ROPE_THETA ** (-jnp.arange(0, d, 2, dtype=jnp.float32) / d)
    ang = jnp.arange(s, dtype=jnp.float32)[:, None] * inv_freq[None, :]
    cos = jnp.cos(ang)[None, :, None, :].astype(x.dtype)
    sin = jnp.sin(ang)[None, :, None, :].astype(x.dtype)
    x1, x2 = x[..., : d // 2], x[..., d // 2:]
    return jnp.concatenate([x1 * cos - x2 * sin, x2 * cos + x1 * sin], axis=-1)


def window_attention(q, k, v, sink):
    b, s, _, d = q.shape
    nb = s // BLOCK
    qb = q.reshape(b, nb, BLOCK, N_KV_HEADS, KV_GROUP, d)
    pad = ((0, 0), (BLOCK, BLOCK), (0, 0), (0, 0))
    kp = jnp.pad(k, pad).reshape(b, nb + 2, BLOCK, N_KV_HEADS, d)
    vp = jnp.pad(v, pad).reshape(b, nb + 2, BLOCK, N_KV_HEADS, d)
    kb = jnp.concatenate([kp[:, :-2], kp[:, 1:-1], kp[:, 2:]], axis=2)
    vb = jnp.concatenate([vp[:, :-2], vp[:, 1:-1], vp[:, 2:]], axis=2)
    scores = jnp.einsum('bnqhgd,bnkhd->bnhgqk', qb, kb).astype(jnp.float32) * (d ** -0.5)
    qpos = jnp.arange(s).reshape(nb, BLOCK)
    kpos = qpos[:, :1] - BLOCK + jnp.arange(3 * BLOCK)[None, :]
    rel = kpos[:, None, :] - qpos[:, :, None]
    valid = (jnp.abs(rel) <= WINDOW) & (kpos[:, None, :] >= 0) & (kpos[:, None, :] < s)
    scores = jnp.where(valid[None, :, None, None], scores, -jnp.inf)
    sink_logit = sink.astype(jnp.float32).reshape(N_KV_HEADS, KV_GROUP)[None, None, :, :, None, None]
    sink_logit = jnp.broadcast_to(sink_logit, scores.shape[:-1] + (1,))
    probs = jax.nn.softmax(jnp.concatenate([scores, sink_logit], axis=-1), axis=-1)[..., :-1]
    out = jnp.einsum('bnhgqk,bnkhd->bnqhgd', probs.astype(v.dtype), vb)
    return out.reshape(b, s, N_Q_HEADS * d)


def memory_attention(q, mk, mv):
    b, s = q.shape[0], q.shape[1]
    scores = jnp.einsum('bshd,bmhd->bhsm', q, mk).astype(jnp.float32) * (X_HEAD_DIM ** -0.5)
    probs = jax.nn.softmax(scores, axis=-1)
    out = jnp.einsum('bhsm,bmhd->bshd', probs.astype(mv.dtype), mv)
    return out.reshape(b, s, X_WIDTH)


def short_conv(z, conv_w):
    s = z.shape[1]
    half = CONV_K // 2
    zp = jnp.pad(z, ((0, 0), (half, half), (0, 0)))
    return sum(conv_w[t] * zp[:, t:t + s] for t in range(CONV_K))


def encoder_layer(x, mem, norm_in, w_in, sink, conv_w, norm_mem, w_mem_kv,
                  g_attn, g_conv, g_mem, w_out):
    b, s, _ = x.shape
    h = rmsnorm(x, norm_in)
    p = jnp.einsum('bsd,de->bse', h, w_in)
    q, k, v, gate_a, conv_b, conv_c, conv_h, gate_c, mq, gate_m = split_cols(p)
    q = rope(q.reshape(b, s, N_Q_HEADS, HEAD_DIM))
    k = rope(k.reshape(b, s, N_KV_HEADS, HEAD_DIM))
    v = v.reshape(b, s, N_KV_HEADS, HEAD_DIM)
    attn = rmsnorm(window_attention(q, k, v, sink), g_attn) * jax.nn.silu(gate_a)
    conv = conv_b * short_conv(conv_c * conv_h, conv_w)
    conv = rmsnorm(conv, g_conv) * jax.nn.silu(gate_c)
    mkv = jnp.einsum('bmd,de->bme', rmsnorm(mem, norm_mem), w_mem_kv)
    mk = mkv[..., :X_WIDTH].reshape(b, N_MEM, N_X_HEADS, X_HEAD_DIM)
    mv = mkv[..., X_WIDTH:].reshape(b, N_MEM, N_X_HEADS, X_HEAD_DIM)
    xo = memory_attention(mq.reshape(b, s, N_X_HEADS, X_HEAD_DIM), mk, mv)
    xo = rmsnorm(xo, g_mem) * jax.nn.silu(gate_m)
    mixed = jnp.concatenate([attn, conv, xo], axis=-1)
    return x + jnp.einsum('bse,ed->bsd', mixed, w_out)


def trunk(x, mem, norm_in, w_in, attn_sink, conv_w, norm_mem, w_mem_kv,
          g_attn, g_conv, g_mem, w_out, final_norm):
    for l in range(DEPTH):
        x = encoder_layer(x, mem, norm_in[l], w_in[l], attn_sink[l], conv_w[l], norm_mem[l],
                          w_mem_kv[l], g_attn[l], g_conv[l], g_mem[l], w_out[l])
    return rmsnorm(x, final_norm)


def setup_inputs(seed: int = 0) -> dict:
    key = jax.random.key(seed)
    ks = jax.random.split(key, 16)
    f32 = jnp.float32
    nrm = lambda k, shape, scale: jax.random.normal(k, shape, f32) * scale
    return {
        "x_prompt": nrm(ks[0], (BATCH, SEQ, D_MODEL), 1.0),
        "x_sample": nrm(ks[1], (DEC_BATCH, DEC_SEQ, D_MODEL), 1.0),
        "mem_prompt": nrm(ks[2], (BATCH, N_MEM, D_MODEL), 1.0),
        "mem_sample": nrm(ks[3], (DEC_BATCH, N_MEM, D_MODEL), 1.0),
        "norm_in": 1.0 + nrm(ks[4], (DEPTH, D_MODEL), 0.02),
        "w_in": nrm(ks[5], (DEPTH, D_MODEL, D_IN), D_MODEL ** -0.5),
        "attn_sink": nrm(ks[6], (DEPTH, N_Q_HEADS), 0.5),
        "conv_w": nrm(ks[7], (DEPTH, CONV_K, CONV_WIDTH), CONV_K ** -0.5),
        "norm_mem": 1.0 + nrm(ks[8], (DEPTH, D_MODEL), 0.02),
        "w_mem_kv": nrm(ks[9], (DEPTH, D_MODEL, 2 * X_WIDTH), D_MODEL ** -0.5),
        "g_attn": 1.0 + nrm(ks[10], (DEPTH, ATTN_WIDTH), 0.02),
        "g_conv": 1.0 + nrm(ks[11], (DEPTH, CONV_WIDTH), 0.02),
        "g_mem": 1.0 + nrm(ks[12], (DEPTH, X_WIDTH), 0.02),
        "w_out": nrm(ks[13], (DEPTH, D_MIX, D_MODEL), D_MIX ** -0.5),
        "final_norm": 1.0 + nrm(ks[14], (D_MODEL,), 0.02),
    }


def reference(x_prompt, x_sample, mem_prompt, mem_sample, norm_in, w_in, attn_sink, conv_w,
              norm_mem, w_mem_kv, g_attn, g_conv, g_mem, w_out, final_norm):
    y_prompt = trunk(x_prompt, mem_prompt, norm_in, w_in, attn_sink, conv_w, norm_mem, w_mem_kv,
                     g_attn, g_conv, g_mem, w_out, final_norm)
    y_sample = trunk(x_sample, mem_sample, norm_in, w_in, attn_sink, conv_w, norm_mem, w_mem_kv,
                     g_attn, g_conv, g_mem, w_out, final_norm)
    return (y_prompt, y_sample)
```

```python
import numpy as np
from contextlib import ExitStack
import concourse.bass as bass
import concourse.mybir as mybir
from concourse.bass_utils import run_bass_kernel_spmd

F32 = mybir.dt.float32
BF16 = mybir.dt.bfloat16
I32 = mybir.dt.int32
ALU = mybir.AluOpType
AF = mybir.ActivationFunctionType

D = 2048
DIN = 5632
NMEM = 256
EPS = 1e-6
MAGIC = 0x5F3759DF
R = 6
R5 = 5
TB = 4
NWS = 2
NEG = -30000.0
ILV = 2


class Buf:
    __slots__ = ("w", "r")

    def __init__(self):
        self.w = None
        self.r = {}


class Eng:
    def __init__(self, eng, sem):
        self.eng, self.sem, self.cnt = eng, sem, 0
        self.seen = {}
        self.pr, self.pw = [], []

    def wait(self, tok):
        if tok is None:
            return
        sem, v = tok
        if self.seen.get(id(sem), 0) >= v:
            return
        self.seen[id(sem)] = v
        self.eng.wait_ge(sem, v)

    def _deps(self, reads, writes):
        for b in reads:
            self.wait(b.w)
        for b in writes:
            self.wait(b.w)
            for t in list(b.r.values()):
                self.wait(t)

    def op(self, fn, reads=(), writes=(), signal=True):
        self._deps(reads, writes)
        inst = fn(self.eng)
        self.pr.extend(reads)
        self.pw.extend(writes)
        if signal:
            self.cnt += 1
            inst.then_inc(self.sem, 1)
            tok = (self.sem, self.cnt)
            for b in self.pr:
                b.r[id(self.sem)] = tok
            for b in self.pw:
                b.w = tok
                b.r = {}
            self.pr, self.pw = [], []
            return tok
        return None


class DSem:
    def __init__(self, sem):
        self.sem, self.cnt = sem, 0


class Dq(Eng):
    def dma(self, out, in_, ds, reads=(), writes=()):
        self._deps(reads, writes)
        self.eng.dma_start(out=out, in_=in_).then_inc(ds.sem, 16)
        ds.cnt += 16
        tok = (ds.sem, ds.cnt)
        for b in reads:
            b.r[id(ds.sem)] = tok
        for b in writes:
            b.w = tok
            b.r = {}
        return tok


class _Stop(Exception):
    pass


def build_program(NB0=36, NL=2, stop=None):
    nc = bass.Bass("TRN2", target_bir_lowering=False)
    NBO = NB0 - 2 * NL

    def dram(name, shape, dt=F32, kind="ExternalInput"):
        return nc.dram_tensor(name, list(shape), dt, kind=kind).ap()

    xin = dram("xin", [NB0 * 128, D])
    memin = dram("memin", [NMEM, D])
    cst = dram("cst", [128, NB0, 128])
    kbd = dram("kb", [128, NB0])
    cnst = dram("cnst", [128, 384])
    gin_d = dram("gin", [NL, 128, D])
    gmem_d = dram("gmem", [NL, 128, D])
    gfin_d = dram("gfin", [128, D])
    cw_d = dram("cw", [NL, 128, 1536])
    sink_d = dram("sink", [NL, 128, 16])
    gcol_d = dram("gcol", [128, NL * 16])
    w_in = dram("w_in", [NL, D, DIN])
    w_out = dram("w_out", [NL, D, D])
    w_mkv = dram("w_mkv", [NL, D, 1024])
    yout = dram("y", [NBO * 128, D], kind="ExternalOutput")
    xs = [dram(f"xs{l}", [(NB0 - 2 * (l + 1)) * 128, D], kind="Internal") for l in range(NL - 1)]
    zsd = dram("zsd", [NB0 * 128 + 2, 512], F32, kind="Internal")
    wos = dram("wos", [NL, 4, 128, 8192], BF16, kind="Internal")
    wis = dram("wis", [NL, 11, 128, 8192], BF16, kind="Internal")

    with ExitStack() as es:
        def sb(name, shape, dt):
            return es.enter_context(nc.sbuf_tensor(name, list(shape), dt))

        def ps(name, shape, dt):
            return es.enter_context(nc.psum_tensor(name, list(shape), dt))

        def sem(name):
            return es.enter_context(nc.semaphore(name))

        wbuf = [sb(f"wbuf{i}", [128, 16, 512], BF16) for i in range(NWS)]
        hT = sb("hT", [128, 16, 512], BF16)
        xbuf = [sb(f"xbuf{i}", [128, D], F32) for i in range(2)]
        hs = sb("hs", [128, D], BF16)
        mixed = hs
        junk = sb("junk", [128, D], BF16)
        qr = [sb(f"qr{i}", [128, 1024], BF16) for i in range(R5)]
        kTr = [sb(f"kTr{i}", [64, 512], BF16) for i in range(R)]
        vr = [sb(f"vr{i}", [128, 4, 65], BF16) for i in range(R)]
        gsr = [sb(f"gsr{i}", [128, D], BF16) for i in range(R5)]
        Br = [sb(f"Br{i}", [128, 512], F32) for i in range(R5)]
        zr = [sb(f"zr{i}", [128, 512], F32) for i in range(R)]
        mqr = [sb(f"mqr{i}", [128, 512], BF16) for i in range(R5)]
        scr = [sb(f"scr{i}", [128, 8], F32) for i in range(R)]
        krope = sb("krope", [128, 256], BF16)
        qT = sb("qT", [64, 2048], BF16)
        mqT = sb("mqT", [128, 512], BF16)
        PT2 = [sb(f"PT{i}", [128, 3, 512], BF16) for i in range(2)]
        attn = sb("attn", [128, 1024], F32)
        xo = sb("xo", [128, 512], F32)
        zm2 = [sb(f"zm{i}", [128, 512], F32) for i in range(2)]
        zp2 = [sb(f"zp{i}", [128, 512], F32) for i in range(2)]
        cva = sb("cva", [128, 512], F32)
        cvb = sb("cvb", [128, 512], F32)
        tnh, r1, r2 = xo, cva, cvb
        cn32 = attn[:, 0:384]
        xa = sb("xa", [128, D], F32)
        ident = sb("ident", [128, 128], BF16)
        triL = sb("triL", [128, 128], BF16)
        triU = sb("triU", [128, 128], BF16)
        tab = [sb("tab0", [128, TB, 128], F32)]
        gin = sb("gin_s", [128, D], F32)
        gfin = sb("gfin_s", [128, D], F32)
        cw = sb("cw_s", [128, 1536], F32)
        esink = sb("esink", [128, 16], F32)
        kb = sb("kb_s", [128, NB0], F32)
        gcol = sb("gcol_s", [128, NL * 16], F32)
        mkT = sb("mkT", [128, 4, 256], BF16)
        mvx = sb("mvx", [128, 2, 4, 129], BF16)
        stA = sb("stA", [128, 8], F32)
        stC = sb("stC", [128, 24], F32)
        den = sb("den", [128, 8], F32)
        mhalf = sb("mhalf", [128, 4], F32)

        mm = [ps(f"mm{i}", [128, 512], F32) for i in range(2)]
        tp = [ps(f"tp{i}", [128, 1024], BF16) for i in range(2)]
        sp = [ps(f"sp{i}", [128, 512], F32) for i in range(3)]
        ob = ps("ob", [128, 512], F32)

        blk = es.enter_context(nc.Block())
        PE = Eng(None, sem("s_pe"))
        ACT = Eng(None, sem("s_act"))
        DVE = Eng(None, sem("s_dve"))
        SP = Dq(None, sem("s_sp"))
        GP = Dq(None, sem("s_gp"))
        ds_w = [DSem(sem(f"d_w{i}")) for i in range(NWS)]
        ds_wg = [DSem(sem(f"d_wg{i}")) for i in range(NWS)]
        ds_x = [DSem(sem(f"d_x{i}")) for i in range(2)]
        ds_xa = DSem(sem("d_xa"))
        ds_psx = [DSem(sem(f"d_psx{i}")) for i in range(2)]
        ds_pss = [DSem(sem(f"d_pss{i}")) for i in range(2)]
        ds_st = [DSem(sem(f"d_st{i}")) for i in range(2)]
        ds_c = DSem(sem("d_c"))
        ds_gf, ds_gin, ds_cw, ds_sk = DSem(sem("d_gf")), DSem(sem("d_gin")), DSem(sem("d_cw")), DSem(sem("d_sk"))
        ds_tab = [DSem(sem(f"d_tab{i}")) for i in range(2)]
        ds_z2 = [DSem(sem("d_z0")), DSem(sem("d_z1"))]
        ds_zs = [DSem(sem(f"d_zs{i}")) for i in range(4)]

        B = lambda: Buf()
        wB = [B() for _ in range(NWS)]
        hTB = [B() for _ in range(TB)]
        xB = [B(), B()]
        xaB = B()
        hsB = B()
        mixedB = hsB
        junkB = B()
        qB = [B() for _ in range(R)]
        kTB = [B() for _ in range(R)]
        vB = [B() for _ in range(R)]
        gsB = [B() for _ in range(R)]
        BB = [B() for _ in range(R)]
        zB = [B() for _ in range(R)]
        mqB = [B() for _ in range(R)]
        scB = [B() for _ in range(R)]
        kropeB = B()
        qTB, mqTB = B(), B()
        PTB2 = [[B() for _ in range(3)] for _ in range(2)]
        attnB, xoB, cvaB, cvbB = B(), B(), B(), B()
        zsB2 = [B(), B()]
        zdB = [B() for _ in range(NB0 + 2)]
        tnhB, r1B, r2B = xoB, cvaB, cvbB
        cB = B()
        tabB = [B(), B()]
        ginB, gfinB, cwB, esinkB = B(), B(), B(), B()
        mkTB, mvxB = B(), B()
        stAB, stCB, denB = B(), B(), B()
        mmB = [B(), B()]
        tpB = [B(), B()]
        spB = [B(), B(), B()]
        obB = B()
        wosB = [[B() for _ in range(16)] for _ in range(NL)]
        wisB = [[B() for _ in range(11)] for _ in range(NL)]
        ds_cv = [[DSem(sem(f"d_cv{l_}_{c_}")) for c_ in range(11)] for l_ in range(NL)]
        xsB = [[B() for _ in range(NB0)] for _ in range(max(NL - 1, 1))]

        def layer_tiles(l):
            lo, hi = l, NB0 - 1 - l
            clo, chi = l + 1, NB0 - 2 - l
            tiles = []
            done = clo
            s = lo
            while s <= hi:
                e = min(s + TB - 1, hi)
                cset = list(range(done, min(e - 1, chi) + 1))
                done = max(done, min(e - 1, chi) + 1)
                tiles.append((list(range(s, e + 1)), cset))
                s = e + 1
            assert done == chi + 1
            return tiles

        sched = []
        for l in range(NL):
            sched += [("mkv", l, 0, False), ("mkv", l, 1, False)]
            for ti, (tb, cset) in enumerate(layer_tiles(l)):
                sched += [("in", l, ci, ti == 0) for ci in range(11)]
                for h0 in range(0, len(cset), 2):
                    sched += [("out", l, oc, False) for oc in range(4)]
        wstate = {"next_issue": 0, "next_use": 0}

        def w_issue():
            k = wstate["next_issue"]
            if k >= len(sched):
                return
            kind, l, ci, first = sched[k]
            s = k % NWS
            flat = wbuf[s][:].rearrange("p c n -> p (c n)")
            if kind == "in" and not first:
                SP.dma(flat, wis[l, ci], ds_w[s], reads=(wisB[l][ci],), writes=(wB[s],))
            elif kind == "out":
                SP.dma(flat, wos[l, ci], ds_w[s], reads=tuple(wosB[l]), writes=(wB[s],))
            else:
                if kind == "in":
                    src = w_in[l][:, ci * 512:(ci + 1) * 512]
                else:
                    src = w_mkv[l][:, ci * 512:(ci + 1) * 512]
                GP.dma(wbuf[s][:], src.rearrange("(c p) n -> p c n", p=128), ds_wg[s], writes=(wB[s],))
                if kind == "in":
                    SP.dma(wis[l, ci], flat, ds_cv[l][ci], reads=(wB[s],), writes=(wisB[l][ci],))
            wstate["next_issue"] = k + 1

        def w_next(kind, l, ci):
            k = wstate["next_use"]
            assert sched[k][:3] == (kind, l, ci), (sched[k], kind, l, ci)
            wstate["next_use"] = k + 1
            while wstate["next_issue"] <= k:
                w_issue()
            return wbuf[k % NWS], wB[k % NWS]

        def w_release():
            while wstate["next_issue"] < min(wstate["next_use"] + NWS - 1 + 1, len(sched)) and \
                    wstate["next_issue"] - wstate["next_use"] < NWS:
                w_issue()

        def rsqrt_pool(t_ap, y_ap, n, tB, yB):
            GP.op(lambda e: e.tensor_tensor(out=y_ap, in0=t_ap, in1=mhalf[:, 0:n], op=ALU.pow),
                  reads=(tB, cB), writes=(yB,))

        def conv_shifts(n, slot):
            zm, zp, zsB, dz = zm2[slot], zp2[slot], zsB2[slot], ds_z2[slot]
            rd = tuple(zdB[i] for i in (n - 1, n, n + 1))
            SP.dma(zm[:], zsd[n * 128:(n + 1) * 128, :], dz, reads=rd, writes=(zsB,))
            SP.dma(zp[:], zsd[n * 128 + 2:(n + 1) * 128 + 2, :], dz, reads=rd, writes=(zsB,))

        def z_store(idx):
            r = idx % R
            SP.dma(zsd[idx * 128 + 1:(idx + 1) * 128 + 1, :], zr[r][:], ds_zs[idx % 4], reads=(zB[r],), writes=(zdB[idx],))

        tpi = [0]

        def transposes(src_fn, n, rows, srcB, consume):
            i = 0
            while i < n:
                cnt = min(8, n - i)
                k = tpi[0] % 2
                tpi[0] += 1
                for j in range(cnt):
                    PE.op(lambda e, i=i, j=j, k=k: e.transpose(tp[k][0:rows, j * 128:(j + 1) * 128], src_fn(i + j), ident[:]),
                          reads=(srcB, cB), writes=(tpB[k],), signal=(j == cnt - 1))
                consume(tp[k], tpB[k], i, cnt)
                i += cnt

        cp_alt = [0]

        def copy_alt(out, in_, reads, writes):
            cp_alt[0] += 1
            if cp_alt[0] % 2:
                ACT.op(lambda e: e.activation(out=out, in_=in_, func=AF.Copy), reads=reads, writes=writes)
            else:
                DVE.op(lambda e: e.tensor_copy(out=out, in_=in_), reads=reads, writes=writes)

        def norm_block_to_hT(xb, xbB, g_ap, gB, j, normalize, rdst=None, rdstB=None, part="all"):
            if part == "T":
                return norm_T(j)
            if not normalize:
                DVE.op(lambda e: e.tensor_tensor(out=hs[:], in0=xb[:], in1=g_ap, op=ALU.mult),
                       reads=(xbB, gB), writes=(hsB,))
            ACT.op(lambda e: e.activation(out=junk[:], in_=xb[:], func=AF.Square, scale=float(D ** -0.5), accum_out=stA[:, 0:1]),
                   reads=(xbB,), writes=(stAB, junkB))
            DVE.op(lambda e: e.tensor_scalar(out=stA[:, 1:2], in0=stA[:, 0:1], scalar1=EPS, scalar2=None, op0=ALU.add),
                   reads=(stAB,), writes=(stAB,))
            if normalize:
                rsqrt_pool(stA[:, 1:2], stA[:, 2:3], 1, stAB, stAB)
                DVE.op(lambda e: e.scalar_tensor_tensor(out=hs[:], in0=xb[:], scalar=stA[:, 2:3], in1=g_ap,
                                                        op0=ALU.mult, op1=ALU.mult),
                       reads=(xbB, stAB, gB), writes=(hsB,))
            else:
                rsqrt_pool(stA[:, 1:2], rdst[:, 0:1], 1, stAB, rdstB)
                DVE.op(lambda e: e.tensor_tensor(out=rdst[:, 1:2], in0=rdst[:, 0:1], in1=rdst[:, 0:1], op=ALU.mult),
                       reads=(rdstB,), writes=(rdstB,))
                DVE.op(lambda e: e.tensor_scalar(out=rdst[:, 2:3], in0=rdst[:, 0:1], scalar1=float(512 ** -0.5), scalar2=None,
                                                 op0=ALU.mult), reads=(rdstB,), writes=(rdstB,))
                DVE.op(lambda e: e.tensor_scalar(out=rdst[:, 3:4], in0=rdst[:, 0:1], scalar1=0.5, scalar2=None,
                                                 op0=ALU.mult), reads=(rdstB,), writes=(rdstB,))
                DVE.op(lambda e: e.tensor_scalar(out=rdst[:, 4:5], in0=rdst[:, 1:2], scalar1=0.5, scalar2=None,
                                                 op0=ALU.mult), reads=(rdstB,), writes=(rdstB,))
                DVE.op(lambda e: e.tensor_scalar(out=rdst[:, 5:6], in0=rdst[:, 0:1], scalar1=0.5, scalar2=None,
                                                 op0=ALU.mult), reads=(rdstB,), writes=(rdstB,))

            if part == "pre":
                return
            norm_T(j)

        def norm_T(j):
            def consume(bank, bankB, i0, cnt):
                copy_alt(hT[:, i0:i0 + cnt, j * 128:(j + 1) * 128],
                         bank[:, 0:cnt * 128].rearrange("p (c t) -> p c t", t=128), (bankB,), (hTB[j],))
            transposes(lambda c: hs[:, c * 128:(c + 1) * 128], 16, 128, hsB, consume)

        streams = {"pe": [], "act": [], "dve": [], "sp": [], "gp": []}

        class Rec:
            def __init__(self, key):
                self.key = key

            def __getattr__(self, name):
                def f(*a, **kw):
                    h = {"inc": None}
                    streams[self.key].append((name, a, kw, h))

                    class _I:
                        def then_inc(_s, sm, v):
                            h["inc"] = (sm, v)
                            return _s
                    return _I()
                return f

        PE.eng, ACT.eng, DVE.eng, SP.eng, GP.eng = Rec("pe"), Rec("act"), Rec("dve"), Rec("sp"), Rec("gp")

        SP.dma(cn32, cnst[:, :], ds_c, writes=(cB, attnB))
        SP.dma(kb[:], kbd[:, :], ds_c, writes=(cB,))
        SP.dma(gcol[:], gcol_d[:, :], ds_c, writes=(cB,))
        SP.dma(gfin[:], gfin_d[:, :], ds_gf, writes=(gfinB,))
        DVE.op(lambda e: e.tensor_copy(out=ident[:], in_=cn32[:, 0:128]), reads=(cB, attnB), writes=(cB,))
        DVE.op(lambda e: e.tensor_copy(out=triL[:], in_=cn32[:, 128:256]), reads=(cB, attnB), writes=(cB,))
        DVE.op(lambda e: e.tensor_copy(out=triU[:], in_=cn32[:, 256:384]), reads=(cB, attnB), writes=(cB,))
        for i in range(R):
            DVE.op(lambda e, i=i: e.memset(vr[i][:], 1.0), writes=(vB[i],))
        DVE.op(lambda e: e.memset(mvx[:], 1.0), writes=(mvxB,))
        DVE.op(lambda e: e.memset(mhalf[:], -0.5), reads=(cB,), writes=(cB,))

        stage = [(hs, hsB), (junk, junkB)]

        def ps_load(l, c, s, q=None, dsx=None):
            q = q or SP
            (q.dma)(xbuf[s][:], w_out[l][c * 128:(c + 1) * 128, :], (dsx or ds_x)[s], writes=(xB[s],))

        def ps_work(l, c, s, q=None, dss=None):
            q = q or SP
            st_t, st_B = stage[s]
            DVE.op(lambda e: e.tensor_scalar(out=st_t[:], in0=xbuf[s][:], scalar1=gcol[:, l * 16 + c:l * 16 + c + 1],
                                             scalar2=None, op0=ALU.mult), reads=(xB[s], cB), writes=(st_B,))
            for oc in range(4):
                (q.dma)(wos[l, oc][:, c * 512:(c + 1) * 512], st_t[:, oc * 512:(oc + 1) * 512], (dss or ds_st)[s],
                        reads=(st_B,), writes=(wosB[l][c],))

        ps_load(0, 0, 0)
        ps_load(0, 1, 1)
        for c in range(16):
            ps_work(0, c, c % 2)
            if c + 2 < 16:
                ps_load(0, c + 2, c % 2)

        def chk(level):
            if stop is not None and level >= stop:
                raise _Stop()

        tile_ctr = [0]
        try:
          chk(0)
          for l in range(NL):
              last = (l == NL - 1)

              def src_rows(idx, l=l):
                  if l == 0:
                      return xin[idx * 128:(idx + 1) * 128, :], None
                  return xs[l - 1][(idx - l) * 128:(idx - l + 1) * 128, :], xsB[l - 1][idx]

              SP.dma(gin[:], gin_d[l], ds_gin, writes=(ginB,))
              SP.dma(cw[:], cw_d[l], ds_cw, writes=(cwB,))
              SP.dma(esink[:], sink_d[l], ds_sk, writes=(esinkB,))
              ACT.op(lambda e: e.activation(out=esink[:], in_=esink[:], func=AF.Exp), reads=(esinkB,), writes=(esinkB,))

              SP.dma(xbuf[1][:], gmem_d[l], ds_x[1], writes=(xB[1],))
              for mb in range(2):
                  SP.dma(xbuf[0][:], memin[mb * 128:(mb + 1) * 128, :], ds_x[0], writes=(xB[0],))
                  norm_block_to_hT(xbuf[0], xB[0], xbuf[1][:], xB[1], mb, True)
              wt, wtB = w_next("mkv", l, 0)
              for h in range(4):
                  k = h % 2
                  for c in range(16):
                      PE.op(lambda e, k=k, c=c, h=h, wt=wt: e.matmul(mm[k][:, 0:256], lhsT=wt[:, c, h * 128:(h + 1) * 128],
                                                                    rhs=hT[:, c, 0:256], start=(c == 0), stop=(c == 15)),
                            reads=(wtB, hTB[0], hTB[1]), writes=(mmB[k],), signal=(c == 15))
                  copy_alt(mkT[:, h, :], mm[k][:, 0:256], (mmB[k],), (mkTB,))
              w_release()
              wt, wtB = w_next("mkv", l, 1)
              for mc in range(2):
                  k = mc % 2
                  for c in range(16):
                      PE.op(lambda e, k=k, c=c, mc=mc, wt=wt: e.matmul(mm[k][:, :], lhsT=hT[:, c, mc * 128:(mc + 1) * 128],
                                                                      rhs=wt[:, c, :], start=(c == 0), stop=(c == 15)),
                            reads=(wtB, hTB[mc]), writes=(mmB[k],), signal=(c == 15))
                  copy_alt(mvx[:, mc, :, 0:128], mm[k][:, :].rearrange("p (h d) -> p h d", d=128), (mmB[k],), (mvxB,))
              w_release()

              chk(1)
              tiles_l = layer_tiles(l)
              preA = {}
              for ti, (tb, cset) in enumerate(tiles_l):
                  ts_ = 0
                  tile_ctr[0] += 1
                  nb = len(tb)
                  if ti not in preA:
                      SP.dma(tab[ts_][:, 0:nb, :], cst[:, tb[0]:tb[0] + nb, :], ds_tab[ts_], writes=(tabB[ts_],))
                  for j, idx in enumerate(tb):
                      if j in preA.get(ti, ()):
                          continue
                      s = j % 2
                      src, srcB = src_rows(idx)
                      SP.dma(xbuf[s][:], src, ds_x[s], reads=(() if srcB is None else (srcB,)), writes=(xB[s],))
                      r = idx % R
                      norm_block_to_hT(xbuf[s], xB[s], gin[:], ginB, j, False, scr[r], scB[r])

                  chk(2)
                  mmi = [0]
                  for ci in range(11):
                      chk(2 + 0.01 * (ci + 1))
                      wt, wtB = w_next("in", l, ci)
                      for j, idx in enumerate(tb):
                          if idx in (l, NB0 - 1 - l) and ci not in (2, 6, 7):
                              continue
                          r = idx % R
                          r5 = idx % R5
                          k = mmi[0] % 2
                          mmi[0] += 1
                          for c in range(16):
                              PE.op(lambda e, k=k, c=c, j=j, wt=wt: e.matmul(mm[k][:, :], lhsT=hT[:, c, j * 128:(j + 1) * 128],
                                                                            rhs=wt[:, c, :], start=(c == 0), stop=(c == 15)),
                                    reads=(wtB, hTB[j]), writes=(mmB[k],), signal=(c == 15))
                          P = mm[k]
                          PB = mmB[k]
                          rstd = scr[r][:, 0:1]
                          rh = scr[r][:, 3:4]
                          rsq = scr[r][:, 1:2]
                          tb_ = tab[ts_]

                          def rope(Pcols, nh, out_ap, outB, P=P, PB=PB, rstd=rstd, j=j, tb_=tb_, r=r):
                              P3 = Pcols.rearrange("p (h d) -> p h d", d=64)
                              r13 = r1[:, 0:nh * 64].rearrange("p (h d) -> p h d", d=64)
                              r23 = r2[:, 0:nh * 64].rearrange("p (h d) -> p h d", d=64)
                              cosb = tb_[:, j, 0:64].unsqueeze(1).to_broadcast([128, nh, 64])
                              nsb = tb_[:, j, 64:96].unsqueeze(1).to_broadcast([128, nh, 32])
                              sb_ = tb_[:, j, 96:128].unsqueeze(1).to_broadcast([128, nh, 32])
                              DVE.op(lambda e: e.scalar_tensor_tensor(out=r13, in0=P3, scalar=rstd, in1=cosb,
                                                                      op0=ALU.mult, op1=ALU.mult),
                                     reads=(PB, scB[r], tabB[ts_]), writes=(r1B,))
                              DVE.op(lambda e: e.scalar_tensor_tensor(out=r23[:, :, 0:32], in0=P3[:, :, 32:64], scalar=rstd,
                                                                      in1=nsb, op0=ALU.mult, op1=ALU.mult),
                                     reads=(PB, scB[r], tabB[ts_]), writes=(r2B,))
                              DVE.op(lambda e: e.scalar_tensor_tensor(out=r23[:, :, 32:64], in0=P3[:, :, 0:32], scalar=rstd,
                                                                      in1=sb_, op0=ALU.mult, op1=ALU.mult),
                                     reads=(PB, scB[r], tabB[ts_]), writes=(r2B,))
                              DVE.op(lambda e: e.tensor_tensor(out=out_ap, in0=r1[:, 0:nh * 64], in1=r2[:, 0:nh * 64], op=ALU.add),
                                     reads=(r1B, r2B), writes=(outB,))

                          if ci in (0, 1):
                              rope(P[:, :], 8, qr[r5][:, ci * 512:(ci + 1) * 512], qB[r5])
                          elif ci == 2:
                              rope(P[:, 0:256], 4, krope[:, :], kropeB)
                              chk(2.031)
                              DVE.op(lambda e, r=r, P=P, rstd=rstd: e.tensor_scalar(
                                  out=vr[r][:, :, 0:64], in0=P[:, 256:512].rearrange("p (g d) -> p g d", d=64),
                                  scalar1=rstd, scalar2=None, op0=ALU.mult), reads=(PB, scB[r]), writes=(vB[r],))

                              chk(2.032)
                              def consume(bank, bankB, i0, cnt, r=r):
                                  chk(2.033)
                                  ACT.op(lambda e: e.activation(out=kTr[r][:, :], in_=bank[0:64, 0:512], func=AF.Copy),
                                         reads=(bankB,), writes=(kTB[r],))
                              transposes(lambda g: krope[:, g * 64:(g + 1) * 64], 4, 64, kropeB, consume)
                          elif ci in (3, 4, 8, 10):
                              off = {3: 0, 4: 512, 8: 1024, 10: 1536}[ci]
                              ACT.op(lambda e, P=P, rh=rh: e.activation(out=tnh[:], in_=P[:, :], func=AF.Tanh, scale=rh),
                                     reads=(PB, scB[r]), writes=(tnhB,))
                              DVE.op(lambda e, P=P, off=off: e.scalar_tensor_tensor(
                                  out=gsr[r5][:, off:off + 512], in0=tnh[:], scalar=1.0, in1=P[:, :], op0=ALU.add, op1=ALU.mult),
                                  reads=(PB, tnhB), writes=(gsB[r5],))
                          elif ci == 5:
                              ACT.op(lambda e, P=P: e.activation(out=Br[r5][:], in_=P[:, :], func=AF.Copy),
                                     reads=(PB,), writes=(BB[r5],))
                          elif ci == 6:
                              ACT.op(lambda e, P=P, r=r, rsq=rsq: e.activation(out=zr[r][:], in_=P[:, :], func=AF.Identity, scale=rsq),
                                     reads=(PB, scB[r]), writes=(zB[r],))
                          elif ci == 7:
                              DVE.op(lambda e, P=P, r=r: e.tensor_tensor(out=zr[r][:], in0=P[:, :], in1=zr[r][:], op=ALU.mult),
                                     reads=(PB, zB[r]), writes=(zB[r],))
                              z_store(idx)
                          elif ci == 9:
                              ACT.op(lambda e, P=P, rstd=rstd: e.activation(out=mqr[r5][:], in_=P[:, :], func=AF.Identity, scale=rstd),
                                     reads=(PB, scB[r]), writes=(mqB[r5],))
                      w_release()
                      if ci == 7 and cset:
                          conv_shifts(cset[0], 0)
                      if ti == min(1, len(tiles_l) - 1) and l + 1 < NL and ci < 9:
                          if ci >= 1:
                              for c_ in (2 * ci - 2, 2 * ci - 1):
                                  ps_work(l + 1, c_, c_ % 2, GP, ds_pss)
                          if ci < 8:
                              for c_ in (2 * ci, 2 * ci + 1):
                                  ps_load(l + 1, c_, c_ % 2, GP, ds_psx)

                  chk(3)
                  def gen_C(n, j, part):
                      r, rp, rn = n % R, (n - 1) % R, (n + 1) % R
                      r5 = n % R5
                      zm, zp, zsB = zm2[j % 2], zp2[j % 2], zsB2[j % 2]

                      def s1(g):
                          pt, ptB = PT2[g % 2], PTB2[g % 2]
                          for jj, kbi in enumerate((n - 1, n, n + 1)):
                              rk = kbi % R
                              PE.op(lambda e: e.matmul(sp[jj][:, :], lhsT=kTr[rk][:, g * 128:(g + 1) * 128],
                                                       rhs=qT[:, g * 512:(g + 1) * 512], start=True, stop=True),
                                    reads=(kTB[rk], qTB), writes=(spB[jj],))
                              ACT.op(lambda e: e.activation(out=pt[:, jj, :], in_=sp[jj][:, :], func=AF.Exp,
                                                            scale=0.125, bias=kb[:, kbi:kbi + 1]),
                                     reads=(spB[jj], cB), writes=(ptB[jj],))
                              if jj != 1:
                                  tri = triL if jj == 0 else triU
                                  DVE.op(lambda e: e.tensor_tensor(
                                      out=pt[:, jj, :].rearrange("p (h q) -> p h q", q=128),
                                      in0=pt[:, jj, :].rearrange("p (h q) -> p h q", q=128),
                                      in1=tri[:].unsqueeze(1).to_broadcast([128, 4, 128]), op=ALU.mult),
                                      reads=(ptB[jj], cB), writes=(ptB[jj],))

                      def s2(g):
                          pt, ptB = PT2[g % 2], PTB2[g % 2]
                          for hh in range(4):
                              for jj, kbi in enumerate((n - 1, n, n + 1)):
                                  rk = kbi % R
                                  PE.op(lambda e: e.matmul(
                                      ob[:, hh * 65:(hh + 1) * 65], lhsT=pt[:, jj, hh * 128:(hh + 1) * 128], rhs=vr[rk][:, g, :],
                                      start=(jj == 0), stop=(jj == 2)),
                                      reads=(ptB[jj], vB[rk]), writes=(obB,), signal=(hh == 3 and jj == 2))
                          obv = ob[:, 0:260].rearrange("p (h e) -> p h e", e=65)
                          DVE.op(lambda e: e.tensor_tensor(out=den[:, 0:4], in0=obv[:, :, 64],
                                                           in1=esink[:, 4 * g:4 * g + 4], op=ALU.add),
                                 reads=(obB, esinkB), writes=(denB,))
                          DVE.op(lambda e: e.reciprocal(out=den[:, 4:8], in_=den[:, 0:4]), reads=(denB,), writes=(denB,))
                          DVE.op(lambda e: e.tensor_tensor(
                              out=attn[:, g * 256:(g + 1) * 256].rearrange("p (h d) -> p h d", d=64), in0=obv[:, :, 0:64],
                              in1=den[:, 4:8].unsqueeze(2).to_broadcast([128, 4, 64]), op=ALU.mult),
                              reads=(obB, denB), writes=(attnB,))

                      def conv():
                          GP.op(lambda e: e.tensor_tensor(out=cva[:], in0=zm[:], in1=cw[:, 0:512], op=ALU.mult),
                                 reads=(zsB, cwB), writes=(cvaB,))
                          GP.op(lambda e: e.tensor_tensor(out=cvb[:], in0=zr[r][:], in1=cw[:, 512:1024], op=ALU.mult),
                                 reads=(zB[r], cwB), writes=(cvbB,))
                          GP.op(lambda e: e.tensor_tensor(out=cva[:], in0=cva[:], in1=cvb[:], op=ALU.add),
                                 reads=(cvaB, cvbB), writes=(cvaB,))
                          GP.op(lambda e: e.tensor_tensor(out=cvb[:], in0=zp[:], in1=cw[:, 1024:1536], op=ALU.mult),
                                 reads=(zsB, cwB), writes=(cvbB,))
                          GP.op(lambda e: e.tensor_tensor(out=cva[:], in0=cva[:], in1=cvb[:], op=ALU.add),
                                 reads=(cvaB, cvbB), writes=(cvaB,))
                          GP.op(lambda e: e.tensor_tensor(out=cva[:], in0=cva[:], in1=Br[r5][:], op=ALU.mult),
                                 reads=(cvaB, BB[r5]), writes=(cvaB,))

                      def mem1():
                          for mc in range(2):
                              for h in range(4):
                                  PE.op(lambda e: e.matmul(sp[mc][:, h * 128:(h + 1) * 128],
                                                           lhsT=mkT[:, h, mc * 128:(mc + 1) * 128],
                                                           rhs=mqT[:, h * 128:(h + 1) * 128], start=True, stop=True),
                                        reads=(mkTB, mqTB), writes=(spB[mc],), signal=(h == 3))
                              ACT.op(lambda e: e.activation(out=PT2[0][:, mc, :], in_=sp[mc][:, :], func=AF.Exp,
                                                            scale=float(128 ** -0.5)),
                                     reads=(spB[mc],), writes=(PTB2[0][mc],))

                      def mem2():
                          for half in range(2):
                              bank, bankB = (ob, obB) if half == 0 else (sp[2], spB[2])
                              for hq in range(2):
                                  h = half * 2 + hq
                                  for mc in range(2):
                                      PE.op(lambda e: e.matmul(
                                          bank[:, hq * 129:(hq + 1) * 129], lhsT=PT2[0][:, mc, h * 128:(h + 1) * 128],
                                          rhs=mvx[:, mc, h, :], start=(mc == 0), stop=(mc == 1)),
                                          reads=(PTB2[0][mc], mvxB), writes=(bankB,), signal=(hq == 1 and mc == 1))
                              bv = bank[:, 0:258].rearrange("p (h e) -> p h e", e=129)
                              DVE.op(lambda e: e.reciprocal(out=den[:, 0:2], in_=bv[:, :, 128]), reads=(bankB,), writes=(denB,))
                              DVE.op(lambda e: e.tensor_tensor(
                                  out=xo[:, half * 256:(half + 1) * 256].rearrange("p (h d) -> p h d", d=128), in0=bv[:, :, 0:128],
                                  in1=den[:, 0:2].unsqueeze(2).to_broadcast([128, 2, 128]), op=ALU.mult),
                                  reads=(bankB, denB), writes=(xoB,))
                          ACT.op(lambda e: e.activation(out=junk[:, 0:512], in_=xo[:], func=AF.Square, scale=float(512 ** -0.5), accum_out=stC[:, 2:3]),
                                 reads=(xoB,), writes=(stCB, junkB))

                      if part == "head":
                          if j + 1 < len(cset):
                              conv_shifts(cset[j + 1], (j + 1) % 2)

                          def consume_q(bank, bankB, i0, cnt):
                              ACT.op(lambda e: e.activation(out=qT[:, i0 * 128:(i0 + cnt) * 128], in_=bank[0:64, 0:cnt * 128],
                                                            func=AF.Copy), reads=(bankB,), writes=(qTB,))
                          transposes(lambda h: qr[r5][:, h * 64:(h + 1) * 64], 16, 64, qB[r5], consume_q)

                          def consume_mq(bank, bankB, i0, cnt):
                              ACT.op(lambda e: e.activation(out=mqT[:, :], in_=bank[:, 0:512], func=AF.Copy),
                                     reads=(bankB,), writes=(mqTB,))
                          transposes(lambda h: mqr[r5][:, h * 128:(h + 1) * 128], 4, 128, mqB[r5], consume_mq)
                          yield
                          s1(0)
                          yield
                          s1(1)
                          yield
                          return

                      if part == "mid":
                          s2(0)
                          yield
                          s1(2)
                          yield
                          s2(1)
                          conv()
                          yield
                          s1(3)
                          yield
                          s2(2)
                          yield
                          s2(3)
                          ACT.op(lambda e: e.activation(out=junk[:, 0:1024], in_=attn[:], func=AF.Square, scale=float(1024 ** -0.5),
                                                        accum_out=stC[:, 0:1]), reads=(attnB,), writes=(stCB, junkB))
                          yield
                          mem1()
                          yield
                          mem2()
                          ACT.op(lambda e: e.activation(out=junk[:, 0:512], in_=cva[:], func=AF.Square, scale=scr[r][:, 2:3],
                                                        accum_out=stC[:, 1:2]), reads=(cvaB, scB[r]), writes=(stCB, junkB))
                          yield
                          return

                      if part == "tailB":
                          def consume_m(bank, bankB, i0, cnt):
                              copy_alt(hT[:, i0:i0 + cnt, j * 128:(j + 1) * 128],
                                       bank[:, 0:cnt * 128].rearrange("p (c t) -> p c t", t=128), (bankB,), (hTB[j],))
                          transposes(lambda c: mixed[:, c * 128:(c + 1) * 128], 16, 128, mixedB, consume_m)
                          yield
                          return

                      assert part == "tailA"
                      DVE.op(lambda e: e.tensor_scalar(out=stC[:, 4:7], in0=stC[:, 0:3], scalar1=EPS, scalar2=None, op0=ALU.add),
                             reads=(stCB,), writes=(stCB,))
                      rsqrt_pool(stC[:, 4:7], stC[:, 8:11], 3, stCB, stCB)
                      GP.op(lambda e: e.tensor_tensor(out=stC[:, 16:19], in0=stC[:, 8:11], in1=scr[r][:, 3:6], op=ALU.mult),
                            reads=(stCB, scB[r]), writes=(stCB,))
                      DVE.op(lambda e: e.scalar_tensor_tensor(out=mixed[:, 0:1024], in0=attn[:], scalar=stC[:, 16:17],
                                                              in1=gsr[r5][:, 0:1024], op0=ALU.mult, op1=ALU.mult),
                             reads=(attnB, stCB, gsB[r5]), writes=(mixedB,))
                      DVE.op(lambda e: e.scalar_tensor_tensor(out=mixed[:, 1024:1536], in0=cva[:], scalar=stC[:, 17:18],
                                                              in1=gsr[r5][:, 1024:1536], op0=ALU.mult, op1=ALU.mult),
                             reads=(cvaB, stCB, gsB[r5]), writes=(mixedB,))
                      DVE.op(lambda e: e.scalar_tensor_tensor(out=mixed[:, 1536:2048], in0=xo[:], scalar=stC[:, 18:19],
                                                              in1=gsr[r5][:, 1536:2048], op0=ALU.mult, op1=ALU.mult),
                             reads=(xoB, stCB, gsB[r5]), writes=(mixedB,))
                      yield

                  def gen_D(half, h0):
                      for k2, n in enumerate(half):
                          src, srcB = src_rows(n)
                          SP.dma(xbuf[k2][:], src, ds_x[k2], reads=(() if srcB is None else (srcB,)), writes=(xB[k2],))
                      for oc in range(4):
                          wt, wtB = w_next("out", l, oc)
                          for k2, n in enumerate(half):
                              j = h0 + k2
                              k = mmi[0] % 2
                              mmi[0] += 1
                              for c in range(16):
                                  PE.op(lambda e: e.matmul(mm[k][:, :], lhsT=hT[:, c, j * 128:(j + 1) * 128],
                                                           rhs=wt[:, c, :], start=(c == 0), stop=(c == 15)),
                                        reads=(wtB, hTB[j]), writes=(mmB[k],), signal=(c == 15))
                              DVE.op(lambda e: e.tensor_tensor(
                                  out=xbuf[k2][:, oc * 512:(oc + 1) * 512], in0=mm[k][:, :], in1=xbuf[k2][:, oc * 512:(oc + 1) * 512],
                                  op=ALU.add), reads=(mmB[k], xB[k2]), writes=(xB[k2],))
                              if k2 == len(half) - 1:
                                  w_release()
                              yield
                      for k2, n in enumerate(half):
                          if not last:
                              SP.dma(xs[l][(n - l - 1) * 128:(n - l) * 128, :], xbuf[k2][:], ds_st[k2],
                                     reads=(xB[k2],), writes=(xsB[l][n],))
                          else:
                              ACT.op(lambda e: e.activation(out=junk[:], in_=xbuf[k2][:], func=AF.Square, scale=float(D ** -0.5),
                                                            accum_out=stA[:, 4:5]), reads=(xB[k2],), writes=(stAB, junkB))
                              DVE.op(lambda e: e.tensor_scalar(out=stA[:, 5:6], in0=stA[:, 4:5], scalar1=EPS, scalar2=None,
                                                               op0=ALU.add), reads=(stAB,), writes=(stAB,))
                              rsqrt_pool(stA[:, 5:6], stA[:, 6:7], 1, stAB, stAB)
                              DVE.op(lambda e: e.scalar_tensor_tensor(out=xbuf[k2][:], in0=xbuf[k2][:], scalar=stA[:, 6:7],
                                                                      in1=gfin[:], op0=ALU.mult, op1=ALU.mult),
                                     reads=(xB[k2], stAB, gfinB), writes=(xB[k2],))
                              SP.dma(yout[(n - NL) * 128:(n - NL + 1) * 128, :], xbuf[k2][:], ds_st[k2], reads=(xB[k2],), writes=())

                  def run(g):
                      for _ in g:
                          pass

                  def chain(*gs):
                      for g in gs:
                          yield from g

                  def interleave(main, fill, every):
                      cnt = 0
                      alive = True
                      for _ in main:
                          cnt += 1
                          if alive and cnt % every == 0:
                              try:
                                  next(fill)
                              except StopIteration:
                                  alive = False
                      if alive:
                          run(fill)

                  nC = len(cset)
                  G = lambda j, part: gen_C(cset[j], j, part)
                  run(G(0, "head"))
                  run(G(0, "mid"))
                  run(G(0, "tailA"))
                  run(G(1, "head"))
                  run(G(0, "tailB"))
                  run(G(1, "mid"))
                  run(G(1, "tailA"))
                  if nC == 4:
                      run(G(2, "head"))
                      run(G(1, "tailB"))
                      interleave(chain(G(2, "mid"), G(2, "tailA"), G(3, "head"), G(2, "tailB"), G(3, "mid"), G(3, "tailA"),
                                       G(3, "tailB")), gen_D(cset[0:2], 0), ILV)
                      chk(4)
                      if ti + 1 < len(tiles_l):
                          tbn = tiles_l[ti + 1][0]

                          def gen_Apre():
                              SP.dma(tab[0][:, 0:len(tbn), :], cst[:, tbn[0]:tbn[0] + len(tbn), :], ds_tab[0], writes=(tabB[0],))
                              for jn in range(min(2, len(tbn))):
                                  idxn = tbn[jn]
                                  srcn, srcBn = src_rows(idxn)
                                  SP.dma(xa[:], srcn, ds_xa, reads=(() if srcBn is None else (srcBn,)), writes=(xaB,))
                                  norm_block_to_hT(xa, xaB, gin[:], ginB, jn, False, scr[idxn % R], scB[idxn % R], part="pre")
                                  yield
                                  norm_T(jn)
                                  yield
                          interleave(gen_D(cset[2:4], 2), gen_Apre(), 1)
                          preA[ti + 1] = tuple(range(min(2, len(tbn))))
                      else:
                          run(gen_D(cset[2:4], 2))
                  else:
                      assert nC == 2
                      run(G(1, "tailB"))
                      chk(4)
                      run(gen_D(cset[0:2], 0))
        except _Stop:
            pass
        if stop is None:
            assert wstate["next_use"] == len(sched)
        final_waits = [(d.sem, d.cnt) for d in ds_st if d.cnt > 0]

        def replay(engine, key, extra_waits=()):
            for (name, a, kw, h) in streams[key]:
                inst = getattr(engine, name)(*a, **kw)
                if h["inc"] is not None:
                    inst.then_inc(*h["inc"])
            for (sm, v) in extra_waits:
                engine.wait_ge(sm, v)

        @blk.tensor
        def _(e):
            replay(e, "pe")

        @blk.scalar
        def _(e):
            replay(e, "act")

        @blk.vector
        def _(e):
            replay(e, "dve")

        @blk.gpsimd
        def _(e):
            replay(e, "gp")

        @blk.sync
        def _(e):
            replay(e, "sp", final_waits)

    return nc


def rope_table(pos):
    inv_freq = (np.float32(10000.0) ** (-(np.arange(0, 64, 2, dtype=np.float32)) / np.float32(64))).astype(np.float32)
    ang = (pos.astype(np.float32)[:, None] * inv_freq[None, :]).astype(np.float32)
    c = np.cos(ang).astype(np.float32)
    s = np.sin(ang).astype(np.float32)
    return np.concatenate([c, c, -s, s], axis=1)


def core_inputs(xseq, a, nown, mem, NB0, NL, shared):
    S = xseq.shape[0]
    halo = 128 * NL
    lo = a - halo
    n = NB0 * 128
    assert n == nown + 2 * halo
    slab = np.zeros((n, D), np.float32)
    s0, s1 = max(lo, 0), min(lo + n, S)
    slab[s0 - lo:s1 - lo] = xseq[s0:s1]
    pos = np.arange(lo, lo + n)
    valid = (pos >= 0) & (pos < S)
    kbv = np.where(valid, 0.0, NEG).astype(np.float32)
    tabv = rope_table(np.maximum(pos, 0))
    d = dict(shared)
    d["xin"] = slab
    d["memin"] = np.ascontiguousarray(mem, dtype=np.float32)
    d["cst"] = np.ascontiguousarray(tabv.reshape(NB0, 128, 128).transpose(1, 0, 2))
    d["kb"] = np.ascontiguousarray(kbv.reshape(NB0, 128).T)
    return d


def shared_inputs(norm_in, w_in, attn_sink, conv_w, norm_mem, w_mem_kv, g_attn, g_conv, g_mem, w_out, final_norm, NL):
    f = lambda a: np.ascontiguousarray(a, dtype=np.float32)
    eye = np.eye(128, dtype=np.float32)
    kk, qq = np.meshgrid(np.arange(128), np.arange(128), indexing="ij")
    cn = np.concatenate([eye, (kk >= qq).astype(np.float32), (kk <= qq).astype(np.float32)], axis=1)
    gcat = np.concatenate([g_attn, g_conv, g_mem], axis=1)[:NL]
    gcol = gcat.reshape(NL, 16, 128).transpose(2, 0, 1).reshape(128, NL * 16)
    bc = lambda v: np.broadcast_to(np.asarray(v, np.float32)[:, None, :], (v.shape[0], 128, v.shape[1]))
    return {
        "cnst": f(cn),
        "gin": f(bc(norm_in[:NL])),
        "gmem": f(bc(norm_mem[:NL])),
        "gfin": f(np.broadcast_to(np.asarray(final_norm, np.float32)[None, :], (128, D))),
        "cw": f(bc(np.asarray(conv_w, np.float32)[:NL].reshape(NL, 1536))),
        "sink": f(bc(attn_sink[:NL])),
        "gcol": f(gcol),
        "w_in": f(w_in[:NL]),
        "w_out": f(w_out[:NL]),
        "w_mkv": f(w_mem_kv[:NL]),
    }


_NC_CACHE = {}


def kernel(x_prompt, x_sample, mem_prompt, mem_sample, norm_in, w_in, attn_sink, conv_w,
           norm_mem, w_mem_kv, g_attn, g_conv, g_mem, w_out, final_norm):
    NL = 2
    NOWN = 4096
    NB0 = (NOWN + 2 * 128 * NL) // 128
    x_prompt = np.asarray(x_prompt, np.float32)
    x_sample = np.asarray(x_sample, np.float32)
    mem_prompt = np.asarray(mem_prompt, np.float32)
    mem_sample = np.asarray(mem_sample, np.float32)
    shared = shared_inputs(np.asarray(norm_in), np.asarray(w_in), np.asarray(attn_sink), np.asarray(conv_w),
                           np.asarray(norm_mem), np.asarray(w_mem_kv), np.asarray(g_attn), np.asarray(g_conv),
                           np.asarray(g_mem), np.asarray(w_out), np.asarray(final_norm), NL)
    assign = []
    for b in range(2):
        for h in range(2):
            assign.append((x_prompt[b], h * NOWN, mem_prompt[b]))
    for q in range(4):
        assign.append((x_sample[0], q * NOWN, mem_sample[0]))
    in_maps = [core_inputs(xs_, a, NOWN, m, NB0, NL, shared) for (xs_, a, m) in assign]
    key = (NB0, NL)
    if key not in _NC_CACHE:
        _NC_CACHE[key] = build_program(NB0, NL)
    nc = _NC_CACHE[key]
    res = run_bass_kernel_spmd(nc, in_maps, core_ids=list(range(8)))
    outs = [np.asarray(r["y"], np.float32) for r in res.results]
    y_prompt = np.stack([np.concatenate(outs[0:2], axis=0), np.concatenate(outs[2:4], axis=0)], axis=0)
    y_sample = np.concatenate(outs[4:8], axis=0)[None]
    return (y_prompt, y_sample)
```

```python
import numpy as np
from contextlib import ExitStack
import concourse.bass as bass
import concourse.mybir as mybir
from concourse.bass_utils import run_bass_kernel_spmd

F32 = mybir.dt.float32
BF16 = mybir.dt.bfloat16
I32 = mybir.dt.int32
ALU = mybir.AluOpType
AF = mybir.ActivationFunctionType

D = 2048
DIN = 5632
NMEM = 256
EPS = 1e-6
MAGIC = 0x5F3759DF
R = 6
R5 = 5
TB = 4
NWS = 2
NEG = -30000.0
ILV = 2


class Buf:
    __slots__ = ("w", "r")

    def __init__(self):
        self.w = None
        self.r = {}


class Eng:
    def __init__(self, eng, sem):
        self.eng, self.sem, self.cnt = eng, sem, 0
        self.seen = {}
        self.pr, self.pw = [], []

    def wait(self, tok):
        if tok is None:
            return
        sem, v = tok
        if self.seen.get(id(sem), 0) >= v:
            return
        self.seen[id(sem)] = v
        self.eng.wait_ge(sem, v)

    def _deps(self, reads, writes):
        for b in reads:
            self.wait(b.w)
        for b in writes:
            self.wait(b.w)
            for t in list(b.r.values()):
                self.wait(t)

    def op(self, fn, reads=(), writes=(), signal=True):
        self._deps(reads, writes)
        inst = fn(self.eng)
        self.pr.extend(reads)
        self.pw.extend(writes)
        if signal:
            self.cnt += 1
            inst.then_inc(self.sem, 1)
            tok = (self.sem, self.cnt)
            for b in self.pr:
                b.r[id(self.sem)] = tok
            for b in self.pw:
                b.w = tok
                b.r = {}
            self.pr, self.pw = [], []
            return tok
        return None


class DSem:
    def __init__(self, sem):
        self.sem, self.cnt = sem, 0


class Dq(Eng):
    def dma(self, out, in_, ds, reads=(), writes=()):
        self._deps(reads, writes)
        self.eng.dma_start(out=out, in_=in_).then_inc(ds.sem, 16)
        ds.cnt += 16
        tok = (ds.sem, ds.cnt)
        for b in reads:
            b.r[id(ds.sem)] = tok
        for b in writes:
            b.w = tok
            b.r = {}
        return tok


class _Stop(Exception):
    pass


def build_program(NB0=36, NL=2, stop=None):
    nc = bass.Bass("TRN2", target_bir_lowering=False)
    NBO = NB0 - 2 * NL

    def dram(name, shape, dt=F32, kind="ExternalInput"):
        return nc.dram_tensor(name, list(shape), dt, kind=kind).ap()

    xin = dram("xin", [NB0 * 128, D])
    memin = dram("memin", [NMEM, D])
    cst = dram("cst", [128, NB0, 128])
    kbd = dram("kb", [128, NB0])
    cnst = dram("cnst", [128, 384])
    gin_d = dram("gin", [NL, 128, D])
    gmem_d = dram("gmem", [NL, 128, D])
    gfin_d = dram("gfin", [128, D])
    cw_d = dram("cw", [NL, 128, 1536])
    sink_d = dram("sink", [NL, 128, 16])
    gcol_d = dram("gcol", [128, NL * 16])
    w_in = dram("w_in", [NL, D, DIN])
    w_out = dram("w_out", [NL, D, D])
    w_mkv = dram("w_mkv", [NL, D, 1024])
    yout = dram("y", [NBO * 128, D], kind="ExternalOutput")
    xs = [dram(f"xs{l}", [(NB0 - 2 * (l + 1)) * 128, D], kind="Internal") for l in range(NL - 1)]
    zsd = dram("zsd", [NB0 * 128 + 2, 512], F32, kind="Internal")
    wos = dram("wos", [NL, 4, 128, 8192], BF16, kind="Internal")
    wis = dram("wis", [NL, 11, 128, 8192], BF16, kind="Internal")

    with ExitStack() as es:
        def sb(name, shape, dt):
            return es.enter_context(nc.sbuf_tensor(name, list(shape), dt))

        def ps(name, shape, dt):
            return es.enter_context(nc.psum_tensor(name, list(shape), dt))

        def sem(name):
            return es.enter_context(nc.semaphore(name))

        wbuf = [sb(f"wbuf{i}", [128, 16, 512], BF16) for i in range(NWS)]
        hT = sb("hT", [128, 16, 512], BF16)
        xbuf = [sb(f"xbuf{i}", [128, D], F32) for i in range(2)]
        hs = sb("hs", [128, D], BF16)
        mixed = hs
        junk = sb("junk", [128, D], BF16)
        qr = [sb(f"qr{i}", [128, 1024], BF16) for i in range(R5)]
        kTr = [sb(f"kTr{i}", [64, 512], BF16) for i in range(R)]
        vr = [sb(f"vr{i}", [128, 4, 65], BF16) for i in range(R)]
        gsr = [sb(f"gsr{i}", [128, D], BF16) for i in range(R5)]
        Br = [sb(f"Br{i}", [128, 512], F32) for i in range(R5)]
        zr = [sb(f"zr{i}", [128, 512], F32) for i in range(R)]
        mqr = [sb(f"mqr{i}", [128, 512], BF16) for i in range(R5)]
        scr = [sb(f"scr{i}", [128, 8], F32) for i in range(R)]
        krope = sb("krope", [128, 256], BF16)
        qT = sb("qT", [64, 2048], BF16)
        mqT = sb("mqT", [128, 512], BF16)
        PT2 = [sb(f"PT{i}", [128, 3, 512], BF16) for i in range(2)]
        attn = sb("attn", [128, 1024], F32)
        xo = sb("xo", [128, 512], F32)
        zm2 = [sb(f"zm{i}", [128, 512], F32) for i in range(2)]
        zp2 = [sb(f"zp{i}", [128, 512], F32) for i in range(2)]
        cva = sb("cva", [128, 512], F32)
        cvb = sb("cvb", [128, 512], F32)
        tnh, r1, r2 = xo, cva, cvb
        cn32 = attn[:, 0:384]
        xa = sb("xa", [128, D], F32)
        ident = sb("ident", [128, 128], BF16)
        triL = sb("triL", [128, 128], BF16)
        triU = sb("triU", [128, 128], BF16)
        tab = [sb("tab0", [128, TB, 128], F32)]
        gin = sb("gin_s", [128, D], F32)
        gfin = sb("gfin_s", [128, D], F32)
        cw = sb("cw_s", [128, 1536], F32)
        esink = sb("esink", [128, 16], F32)
        kb = sb("kb_s", [128, NB0], F32)
        gcol = sb("gcol_s", [128, NL * 16], F32)
        mkT = sb("mkT", [128, 4, 256], BF16)
        mvx = sb("mvx", [128, 2, 4, 129], BF16)
        stA = sb("stA", [128, 8], F32)
        stC = sb("stC", [128, 24], F32)
        den = sb("den", [128, 8], F32)
        mhalf = sb("mhalf", [128, 4], F32)

        mm = [ps(f"mm{i}", [128, 512], F32) for i in range(2)]
        tp = [ps(f"tp{i}", [128, 1024], BF16) for i in range(2)]
        sp = [ps(f"sp{i}", [128, 512], F32) for i in range(3)]
        ob = ps("ob", [128, 512], F32)

        blk = es.enter_context(nc.Block())
        PE = Eng(None, sem("s_pe"))
        ACT = Eng(None, sem("s_act"))
        DVE = Eng(None, sem("s_dve"))
        SP = Dq(None, sem("s_sp"))
        GP = Dq(None, sem("s_gp"))
        ds_w = [DSem(sem(f"d_w{i}")) for i in range(NWS)]
        ds_wg = [DSem(sem(f"d_wg{i}")) for i in range(NWS)]
        ds_x = [DSem(sem(f"d_x{i}")) for i in range(2)]
        ds_xa = DSem(sem("d_xa"))
        ds_psx = [DSem(sem(f"d_psx{i}")) for i in range(2)]
        ds_pss = [DSem(sem(f"d_pss{i}")) for i in range(2)]
        ds_st = [DSem(sem(f"d_st{i}")) for i in range(2)]
        ds_c = DSem(sem("d_c"))
        ds_gf, ds_gin, ds_cw, ds_sk = DSem(sem("d_gf")), DSem(sem("d_gin")), DSem(sem("d_cw")), DSem(sem("d_sk"))
        ds_tab = [DSem(sem(f"d_tab{i}")) for i in range(2)]
        ds_z2 = [DSem(sem("d_z0")), DSem(sem("d_z1"))]
        ds_zs = [DSem(sem(f"d_zs{i}")) for i in range(4)]

        B = lambda: Buf()
        wB = [B() for _ in range(NWS)]
        hTB = [B() for _ in range(TB)]
        xB = [B(), B()]
        xaB = B()
        hsB = B()
        mixedB = hsB
        junkB = B()
        qB = [B() for _ in range(R)]
        kTB = [B() for _ in range(R)]
        vB = [B() for _ in range(R)]
        gsB = [B() for _ in range(R)]
        BB = [B() for _ in range(R)]
        zB = [B() for _ in range(R)]
        mqB = [B() for _ in range(R)]
        scB = [B() for _ in range(R)]
        kropeB = B()
        qTB, mqTB = B(), B()
        PTB2 = [[B() for _ in range(3)] for _ in range(2)]
        attnB, xoB, cvaB, cvbB = B(), B(), B(), B()
        zsB2 = [B(), B()]
        zdB = [B() for _ in range(NB0 + 2)]
        tnhB, r1B, r2B = xoB, cvaB, cvbB
        cB = B()
        tabB = [B(), B()]
        ginB, gfinB, cwB, esinkB = B(), B(), B(), B()
        mkTB, mvxB = B(), B()
        stAB, stCB, denB = B(), B(), B()
        mmB = [B(), B()]
        tpB = [B(), B()]
        spB = [B(), B(), B()]
        obB = B()
        wosB = [[B() for _ in range(16)] for _ in range(NL)]
        wisB = [[B() for _ in range(11)] for _ in range(NL)]
        ds_cv = [[DSem(sem(f"d_cv{l_}_{c_}")) for c_ in range(11)] for l_ in range(NL)]
        xsB = [[B() for _ in range(NB0)] for _ in range(max(NL - 1, 1))]

        def layer_tiles(l):
            lo, hi = l, NB0 - 1 - l
            clo, chi = l + 1, NB0 - 2 - l
            tiles = []
            done = clo
            s = lo
            while s <= hi:
                e = min(s + TB - 1, hi)
                cset = list(range(done, min(e - 1, chi) + 1))
                done = max(done, min(e - 1, chi) + 1)
                tiles.append((list(range(s, e + 1)), cset))
                s = e + 1
            assert done == chi + 1
            return tiles

        sched = []
        for l in range(NL):
            sched += [("mkv", l, 0, False), ("mkv", l, 1, False)]
            for ti, (tb, cset) in enumerate(layer_tiles(l)):
                sched += [("in", l, ci, ti == 0) for ci in range(11)]
                for h0 in range(0, len(cset), 2):
                    sched += [("out", l, oc, False) for oc in range(4)]
        wstate = {"next_issue": 0, "next_use": 0}

        def w_issue():
            k = wstate["next_issue"]
            if k >= len(sched):
                return
            kind, l, ci, first = sched[k]
            s = k % NWS
            flat = wbuf[s][:].rearrange("p c n -> p (c n)")
            if kind == "in" and not first:
                SP.dma(flat, wis[l, ci], ds_w[s], reads=(wisB[l][ci],), writes=(wB[s],))
            elif kind == "out":
                SP.dma(flat, wos[l, ci], ds_w[s], reads=tuple(wosB[l]), writes=(wB[s],))
            else:
                if kind == "in":
                    src = w_in[l][:, ci * 512:(ci + 1) * 512]
                else:
                    src = w_mkv[l][:, ci * 512:(ci + 1) * 512]
                GP.dma(wbuf[s][:], src.rearrange("(c p) n -> p c n", p=128), ds_wg[s], writes=(wB[s],))
                if kind == "in":
                    SP.dma(wis[l, ci], flat, ds_cv[l][ci], reads=(wB[s],), writes=(wisB[l][ci],))
            wstate["next_issue"] = k + 1

        def w_next(kind, l, ci):
            k = wstate["next_use"]
            assert sched[k][:3] == (kind, l, ci), (sched[k], kind, l, ci)
            wstate["next_use"] = k + 1
            while wstate["next_issue"] <= k:
                w_issue()
            return wbuf[k % NWS], wB[k % NWS]

        def w_release():
            while wstate["next_issue"] < min(wstate["next_use"] + NWS - 1 + 1, len(sched)) and \
                    wstate["next_issue"] - wstate["next_use"] < NWS:
                w_issue()

        def rsqrt_pool(t_ap, y_ap, n, tB, yB):
            GP.op(lambda e: e.tensor_tensor(out=y_ap, in0=t_ap, in1=mhalf[:, 0:n], op=ALU.pow),
                  reads=(tB, cB), writes=(yB,))

        def conv_shifts(n, slot):
            zm, zp, zsB, dz = zm2[slot], zp2[slot], zsB2[slot], ds_z2[slot]
            rd = tuple(zdB[i] for i in (n - 1, n, n + 1))
            SP.dma(zm[:], zsd[n * 128:(n + 1) * 128, :], dz, reads=rd, writes=(zsB,))
            SP.dma(zp[:], zsd[n * 128 + 2:(n + 1) * 128 + 2, :], dz, reads=rd, writes=(zsB,))

        def z_store(idx):
            r = idx % R
            SP.dma(zsd[idx * 128 + 1:(idx + 1) * 128 + 1, :], zr[r][:], ds_zs[idx % 4], reads=(zB[r],), writes=(zdB[idx],))

        tpi = [0]

        def transposes(src_fn, n, rows, srcB, consume):
            i = 0
            while i < n:
                cnt = min(8, n - i)
                k = tpi[0] % 2
                tpi[0] += 1
                for j in range(cnt):
                    PE.op(lambda e, i=i, j=j, k=k: e.transpose(tp[k][0:rows, j * 128:(j + 1) * 128], src_fn(i + j), ident[:]),
                          reads=(srcB, cB), writes=(tpB[k],), signal=(j == cnt - 1))
                consume(tp[k], tpB[k], i, cnt)
                i += cnt

        cp_alt = [0]

        def copy_alt(out, in_, reads, writes):
            cp_alt[0] += 1
            if cp_alt[0] % 2:
                ACT.op(lambda e: e.activation(out=out, in_=in_, func=AF.Copy), reads=reads, writes=writes)
            else:
                DVE.op(lambda e: e.tensor_copy(out=out, in_=in_), reads=reads, writes=writes)

        def norm_block_to_hT(xb, xbB, g_ap, gB, j, normalize, rdst=None, rdstB=None, part="all"):
            if part == "T":
                return norm_T(j)
            if not normalize:
                DVE.op(lambda e: e.tensor_tensor(out=hs[:], in0=xb[:], in1=g_ap, op=ALU.mult),
                       reads=(xbB, gB), writes=(hsB,))
            ACT.op(lambda e: e.activation(out=junk[:], in_=xb[:], func=AF.Square, scale=float(D ** -0.5), accum_out=stA[:, 0:1]),
                   reads=(xbB,), writes=(stAB, junkB))
            DVE.op(lambda e: e.tensor_scalar(out=stA[:, 1:2], in0=stA[:, 0:1], scalar1=EPS, scalar2=None, op0=ALU.add),
                   reads=(stAB,), writes=(stAB,))
            if normalize:
                rsqrt_pool(stA[:, 1:2], stA[:, 2:3], 1, stAB, stAB)
                DVE.op(lambda e: e.scalar_tensor_tensor(out=hs[:], in0=xb[:], scalar=stA[:, 2:3], in1=g_ap,
                                                        op0=ALU.mult, op1=ALU.mult),
                       reads=(xbB, stAB, gB), writes=(hsB,))
            else:
                rsqrt_pool(stA[:, 1:2], rdst[:, 0:1], 1, stAB, rdstB)
                DVE.op(lambda e: e.tensor_tensor(out=rdst[:, 1:2], in0=rdst[:, 0:1], in1=rdst[:, 0:1], op=ALU.mult),
                       reads=(rdstB,), writes=(rdstB,))
                DVE.op(lambda e: e.tensor_scalar(out=rdst[:, 2:3], in0=rdst[:, 0:1], scalar1=float(512 ** -0.5), scalar2=None,
                                                 op0=ALU.mult), reads=(rdstB,), writes=(rdstB,))
                DVE.op(lambda e: e.tensor_scalar(out=rdst[:, 3:4], in0=rdst[:, 0:1], scalar1=0.5, scalar2=None,
                                                 op0=ALU.mult), reads=(rdstB,), writes=(rdstB,))
                DVE.op(lambda e: e.tensor_scalar(out=rdst[:, 4:5], in0=rdst[:, 1:2], scalar1=0.5, scalar2=None,
                                                 op0=ALU.mult), reads=(rdstB,), writes=(rdstB,))
                DVE.op(lambda e: e.tensor_scalar(out=rdst[:, 5:6], in0=rdst[:, 0:1], scalar1=0.5, scalar2=None,
                                                 op0=ALU.mult), reads=(rdstB,), writes=(rdstB,))

            if part == "pre":
                return
            norm_T(j)

        def norm_T(j):
            def consume(bank, bankB, i0, cnt):
                copy_alt(hT[:, i0:i0 + cnt, j * 128:(j + 1) * 128],
                         bank[:, 0:cnt * 128].rearrange("p (c t) -> p c t", t=128), (bankB,), (hTB[j],))
            transposes(lambda c: hs[:, c * 128:(c + 1) * 128], 16, 128, hsB, consume)

        streams = {"pe": [], "act": [], "dve": [], "sp": [], "gp": []}

        class Rec:
            def __init__(self, key):
                self.key = key

            def __getattr__(self, name):
                def f(*a, **kw):
                    h = {"inc": None}
                    streams[self.key].append((name, a, kw, h))

                    class _I:
                        def then_inc(_s, sm, v):
                            h["inc"] = (sm, v)
                            return _s
                    return _I()
                return f

        PE.eng, ACT.eng, DVE.eng, SP.eng, GP.eng = Rec("pe"), Rec("act"), Rec("dve"), Rec("sp"), Rec("gp")

        SP.dma(cn32, cnst[:, :], ds_c, writes=(cB, attnB))
        SP.dma(kb[:], kbd[:, :], ds_c, writes=(cB,))
        SP.dma(gcol[:], gcol_d[:, :], ds_c, writes=(cB,))
        SP.dma(gfin[:], gfin_d[:, :], ds_gf, writes=(gfinB,))
        DVE.op(lambda e: e.tensor_copy(out=ident[:], in_=cn32[:, 0:128]), reads=(cB, attnB), writes=(cB,))
        DVE.op(lambda e: e.tensor_copy(out=triL[:], in_=cn32[:, 128:256]), reads=(cB, attnB), writes=(cB,))
        DVE.op(lambda e: e.tensor_copy(out=triU[:], in_=cn32[:, 256:384]), reads=(cB, attnB), writes=(cB,))
        for i in range(R):
            DVE.op(lambda e, i=i: e.memset(vr[i][:], 1.0), writes=(vB[i],))
        DVE.op(lambda e: e.memset(mvx[:], 1.0), writes=(mvxB,))
        DVE.op(lambda e: e.memset(mhalf[:], -0.5), reads=(cB,), writes=(cB,))

        stage = [(hs, hsB), (junk, junkB)]

        def ps_load(l, c, s, q=None, dsx=None):
            q = q or SP
            (q.dma)(xbuf[s][:], w_out[l][c * 128:(c + 1) * 128, :], (dsx or ds_x)[s], writes=(xB[s],))

        def ps_work(l, c, s, q=None, dss=None):
            q = q or SP
            st_t, st_B = stage[s]
            DVE.op(lambda e: e.tensor_scalar(out=st_t[:], in0=xbuf[s][:], scalar1=gcol[:, l * 16 + c:l * 16 + c + 1],
                                             scalar2=None, op0=ALU.mult), reads=(xB[s], cB), writes=(st_B,))
            for oc in range(4):
                (q.dma)(wos[l, oc][:, c * 512:(c + 1) * 512], st_t[:, oc * 512:(oc + 1) * 512], (dss or ds_st)[s],
                        reads=(st_B,), writes=(wosB[l][c],))

        ps_load(0, 0, 0)
        ps_load(0, 1, 1)
        for c in range(16):
            ps_work(0, c, c % 2)
            if c + 2 < 16:
                ps_load(0, c + 2, c % 2)

        def chk(level):
            if stop is not None and level >= stop:
                raise _Stop()

        tile_ctr = [0]
        try:
          chk(0)
          for l in range(NL):
              last = (l == NL - 1)

              def src_rows(idx, l=l):
                  if l == 0:
                      return xin[idx * 128:(idx + 1) * 128, :], None
                  return xs[l - 1][(idx - l) * 128:(idx - l + 1) * 128, :], xsB[l - 1][idx]

              SP.dma(gin[:], gin_d[l], ds_gin, writes=(ginB,))
              SP.dma(cw[:], cw_d[l], ds_cw, writes=(cwB,))
              SP.dma(esink[:], sink_d[l], ds_sk, writes=(esinkB,))
              ACT.op(lambda e: e.activation(out=esink[:], in_=esink[:], func=AF.Exp), reads=(esinkB,), writes=(esinkB,))

              SP.dma(xbuf[1][:], gmem_d[l], ds_x[1], writes=(xB[1],))
              for mb in range(2):
                  SP.dma(xbuf[0][:], memin[mb * 128:(mb + 1) * 128, :], ds_x[0], writes=(xB[0],))
                  norm_block_to_hT(xbuf[0], xB[0], xbuf[1][:], xB[1], mb, True)
              wt, wtB = w_next("mkv", l, 0)
              for h in range(4):
                  k = h % 2
                  for c in range(16):
                      PE.op(lambda e, k=k, c=c, h=h, wt=wt: e.matmul(mm[k][:, 0:256], lhsT=wt[:, c, h * 128:(h + 1) * 128],
                                                                    rhs=hT[:, c, 0:256], start=(c == 0), stop=(c == 15)),
                            reads=(wtB, hTB[0], hTB[1]), writes=(mmB[k],), signal=(c == 15))
                  copy_alt(mkT[:, h, :], mm[k][:, 0:256], (mmB[k],), (mkTB,))
              w_release()
              wt, wtB = w_next("mkv", l, 1)
              for mc in range(2):
                  k = mc % 2
                  for c in range(16):
                      PE.op(lambda e, k=k, c=c, mc=mc, wt=wt: e.matmul(mm[k][:, :], lhsT=hT[:, c, mc * 128:(mc + 1) * 128],
                                                                      rhs=wt[:, c, :], start=(c == 0), stop=(c == 15)),
                            reads=(wtB, hTB[mc]), writes=(mmB[k],), signal=(c == 15))
                  copy_alt(mvx[:, mc, :, 0:128], mm[k][:, :].rearrange("p (h d) -> p h d", d=128), (mmB[k],), (mvxB,))
              w_release()

              chk(1)
              tiles_l = layer_tiles(l)
              preA = {}
              for ti, (tb, cset) in enumerate(tiles_l):
                  ts_ = 0
                  tile_ctr[0] += 1
                  nb = len(tb)
                  if ti not in preA:
                      SP.dma(tab[ts_][:, 0:nb, :], cst[:, tb[0]:tb[0] + nb, :], ds_tab[ts_], writes=(tabB[ts_],))
                  for j, idx in enumerate(tb):
                      if j in preA.get(ti, ()):
                          continue
                      s = j % 2
                      src, srcB = src_rows(idx)
                      SP.dma(xbuf[s][:], src, ds_x[s], reads=(() if srcB is None else (srcB,)), writes=(xB[s],))
                      r = idx % R
                      norm_block_to_hT(xbuf[s], xB[s], gin[:], ginB, j, False, scr[r], scB[r])

                  chk(2)
                  mmi = [0]
                  for ci in range(11):
                      chk(2 + 0.01 * (ci + 1))
                      wt, wtB = w_next("in", l, ci)
                      for j, idx in enumerate(tb):
                          if idx in (l, NB0 - 1 - l) and ci not in (2, 6, 7):
                              continue
                          r = idx % R
                          r5 = idx % R5
                          k = mmi[0] % 2
                          mmi[0] += 1
                          for c in range(16):
                              PE.op(lambda e, k=k, c=c, j=j, wt=wt: e.matmul(mm[k][:, :], lhsT=hT[:, c, j * 128:(j + 1) * 128],
                                                                            rhs=wt[:, c, :], start=(c == 0), stop=(c == 15)),
                                    reads=(wtB, hTB[j]), writes=(mmB[k],), signal=(c == 15))
                          P = mm[k]
                          PB = mmB[k]
                          rstd = scr[r][:, 0:1]
                          rh = scr[r][:, 3:4]
                          rsq = scr[r][:, 1:2]
                          tb_ = tab[ts_]

                          def rope(Pcols, nh, out_ap, outB, P=P, PB=PB, rstd=rstd, j=j, tb_=tb_, r=r):
                              P3 = Pcols.rearrange("p (h d) -> p h d", d=64)
                              r13 = r1[:, 0:nh * 64].rearrange("p (h d) -> p h d", d=64)
                              r23 = r2[:, 0:nh * 64].rearrange("p (h d) -> p h d", d=64)
                              cosb = tb_[:, j, 0:64].unsqueeze(1).to_broadcast([128, nh, 64])
                              nsb = tb_[:, j, 64:96].unsqueeze(1).to_broadcast([128, nh, 32])
                              sb_ = tb_[:, j, 96:128].unsqueeze(1).to_broadcast([128, nh, 32])
                              DVE.op(lambda e: e.scalar_tensor_tensor(out=r13, in0=P3, scalar=rstd, in1=cosb,
                                                                      op0=ALU.mult, op1=ALU.mult),
                                     reads=(PB, scB[r], tabB[ts_]), writes=(r1B,))
                              DVE.op(lambda e: e.scalar_tensor_tensor(out=r23[:, :, 0:32], in0=P3[:, :, 32:64], scalar=rstd,
                                                                      in1=nsb, op0=ALU.mult, op1=ALU.mult),
                                     reads=(PB, scB[r], tabB[ts_]), writes=(r2B,))
                              DVE.op(lambda e: e.scalar_tensor_tensor(out=r23[:, :, 32:64], in0=P3[:, :, 0:32], scalar=rstd,
                                                                      in1=sb_, op0=ALU.mult, op1=ALU.mult),
                                     reads=(PB, scB[r], tabB[ts_]), writes=(r2B,))
                              DVE.op(lambda e: e.tensor_tensor(out=out_ap, in0=r1[:, 0:nh * 64], in1=r2[:, 0:nh * 64], op=ALU.add),
                                     reads=(r1B, r2B), writes=(outB,))

                          if ci in (0, 1):
                              rope(P[:, :], 8, qr[r5][:, ci * 512:(ci + 1) * 512], qB[r5])
                          elif ci == 2:
                              rope(P[:, 0:256], 4, krope[:, :], kropeB)
                              chk(2.031)
                              DVE.op(lambda e, r=r, P=P, rstd=rstd: e.tensor_scalar(
                                  out=vr[r][:, :, 0:64], in0=P[:, 256:512].rearrange("p (g d) -> p g d", d=64),
                                  scalar1=rstd, scalar2=None, op0=ALU.mult), reads=(PB, scB[r]), writes=(vB[r],))

                              chk(2.032)
                              def consume(bank, bankB, i0, cnt, r=r):
                                  chk(2.033)
                                  ACT.op(lambda e: e.activation(out=kTr[r][:, :], in_=bank[0:64, 0:512], func=AF.Copy),
                                         reads=(bankB,), writes=(kTB[r],))
                              transposes(lambda g: krope[:, g * 64:(g + 1) * 64], 4, 64, kropeB, consume)
                          elif ci in (3, 4, 8, 10):
                              off = {3: 0, 4: 512, 8: 1024, 10: 1536}[ci]
                              ACT.op(lambda e, P=P, rh=rh: e.activation(out=tnh[:], in_=P[:, :], func=AF.Tanh, scale=rh),
                                     reads=(PB, scB[r]), writes=(tnhB,))
                              DVE.op(lambda e, P=P, off=off: e.scalar_tensor_tensor(
                                  out=gsr[r5][:, off:off + 512], in0=tnh[:], scalar=1.0, in1=P[:, :], op0=ALU.add, op1=ALU.mult),
                                  reads=(PB, tnhB), writes=(gsB[r5],))
                          elif ci == 5:
                              ACT.op(lambda e, P=P: e.activation(out=Br[r5][:], in_=P[:, :], func=AF.Copy),
                                     reads=(PB,), writes=(BB[r5],))
                          elif ci == 6:
                              ACT.op(lambda e, P=P, r=r, rsq=rsq: e.activation(out=zr[r][:], in_=P[:, :], func=AF.Identity, scale=rsq),
                                     reads=(PB, scB[r]), writes=(zB[r],))
                          elif ci == 7:
                              DVE.op(lambda e, P=P, r=r: e.tensor_tensor(out=zr[r][:], in0=P[:, :], in1=zr[r][:], op=ALU.mult),
                                     reads=(PB, zB[r]), writes=(zB[r],))
                              z_store(idx)
                          elif ci == 9:
                              ACT.op(lambda e, P=P, rstd=rstd: e.activation(out=mqr[r5][:], in_=P[:, :], func=AF.Identity, scale=rstd),
                                     reads=(PB, scB[r]), writes=(mqB[r5],))
                      w_release()
                      if ci == 7 and cset:
                          conv_shifts(cset[0], 0)
                      if ti == min(1, len(tiles_l) - 1) and l + 1 < NL and ci < 9:
                          if ci >= 1:
                              for c_ in (2 * ci - 2, 2 * ci - 1):
                                  ps_work(l + 1, c_, c_ % 2, GP, ds_pss)
                          if ci < 8:
                              for c_ in (2 * ci, 2 * ci + 1):
                                  ps_load(l + 1, c_, c_ % 2, GP, ds_psx)

                  chk(3)
                  def gen_C(n, j, part):
                      r, rp, rn = n % R, (n - 1) % R, (n + 1) % R
                      r5 = n % R5
                      zm, zp, zsB = zm2[j % 2], zp2[j % 2], zsB2[j % 2]

                      def s1(g):
                          pt, ptB = PT2[g % 2], PTB2[g % 2]
                          for jj, kbi in enumerate((n - 1, n, n + 1)):
                              rk = kbi % R
                              PE.op(lambda e: e.matmul(sp[jj][:, :], lhsT=kTr[rk][:, g * 128:(g + 1) * 128],
                                                       rhs=qT[:, g * 512:(g + 1) * 512], start=True, stop=True),
                                    reads=(kTB[rk], qTB), writes=(spB[jj],))
                              ACT.op(lambda e: e.activation(out=pt[:, jj, :], in_=sp[jj][:, :], func=AF.Exp,
                                                            scale=0.125, bias=kb[:, kbi:kbi + 1]),
                                     reads=(spB[jj], cB), writes=(ptB[jj],))
                              if jj != 1:
                                  tri = triL if jj == 0 else triU
                                  DVE.op(lambda e: e.tensor_tensor(
                                      out=pt[:, jj, :].rearrange("p (h q) -> p h q", q=128),
                                      in0=pt[:, jj, :].rearrange("p (h q) -> p h q", q=128),
                                      in1=tri[:].unsqueeze(1).to_broadcast([128, 4, 128]), op=ALU.mult),
                                      reads=(ptB[jj], cB), writes=(ptB[jj],))

                      def s2(g):
                          pt, ptB = PT2[g % 2], PTB2[g % 2]
                          for hh in range(4):
                              for jj, kbi in enumerate((n - 1, n, n + 1)):
                                  rk = kbi % R
                                  PE.op(lambda e: e.matmul(
                                      ob[:, hh * 65:(hh + 1) * 65], lhsT=pt[:, jj, hh * 128:(hh + 1) * 128], rhs=vr[rk][:, g, :],
                                      start=(jj == 0), stop=(jj == 2)),
                                      reads=(ptB[jj], vB[rk]), writes=(obB,), signal=(hh == 3 and jj == 2))
                          obv = ob[:, 0:260].rearrange("p (h e) -> p h e", e=65)
                          DVE.op(lambda e: e.tensor_tensor(out=den[:, 0:4], in0=obv[:, :, 64],
                                                           in1=esink[:, 4 * g:4 * g + 4], op=ALU.add),
                                 reads=(obB, esinkB), writes=(denB,))
                          DVE.op(lambda e: e.reciprocal(out=den[:, 4:8], in_=den[:, 0:4]), reads=(denB,), writes=(denB,))
                          DVE.op(lambda e: e.tensor_tensor(
                              out=attn[:, g * 256:(g + 1) * 256].rearrange("p (h d) -> p h d", d=64), in0=obv[:, :, 0:64],
                              in1=den[:, 4:8].unsqueeze(2).to_broadcast([128, 4, 64]), op=ALU.mult),
                              reads=(obB, denB), writes=(attnB,))

                      def conv():
                          GP.op(lambda e: e.tensor_tensor(out=cva[:], in0=zm[:], in1=cw[:, 0:512], op=ALU.mult),
                                 reads=(zsB, cwB), writes=(cvaB,))
                          GP.op(lambda e: e.tensor_tensor(out=cvb[:], in0=zr[r][:], in1=cw[:, 512:1024], op=ALU.mult),
                                 reads=(zB[r], cwB), writes=(cvbB,))
                          GP.op(lambda e: e.tensor_tensor(out=cva[:], in0=cva[:], in1=cvb[:], op=ALU.add),
                                 reads=(cvaB, cvbB), writes=(cvaB,))
                          GP.op(lambda e: e.tensor_tensor(out=cvb[:], in0=zp[:], in1=cw[:, 1024:1536], op=ALU.mult),
                                 reads=(zsB, cwB), writes=(cvbB,))
                          GP.op(lambda e: e.tensor_tensor(out=cva[:], in0=cva[:], in1=cvb[:], op=ALU.add),
                                 reads=(cvaB, cvbB), writes=(cvaB,))
                          GP.op(lambda e: e.tensor_tensor(out=cva[:], in0=cva[:], in1=Br[r5][:], op=ALU.mult),
                                 reads=(cvaB, BB[r5]), writes=(cvaB,))

                      def mem1():
                          for mc in range(2):
                              for h in range(4):
                                  PE.op(lambda e: e.matmul(sp[mc][:, h * 128:(h + 1) * 128],
                                                           lhsT=mkT[:, h, mc * 128:(mc + 1) * 128],
                                                           rhs=mqT[:, h * 128:(h + 1) * 128], start=True, stop=True),
                                        reads=(mkTB, mqTB), writes=(spB[mc],), signal=(h == 3))
                              ACT.op(lambda e: e.activation(out=PT2[0][:, mc, :], in_=sp[mc][:, :], func=AF.Exp,
                                                            scale=float(128 ** -0.5)),
                                     reads=(spB[mc],), writes=(PTB2[0][mc],))

                      def mem2():
                          for half in range(2):
                              bank, bankB = (ob, obB) if half == 0 else (sp[2], spB[2])
                              for hq in range(2):
                                  h = half * 2 + hq
                                  for mc in range(2):
                                      PE.op(lambda e: e.matmul(
                                          bank[:, hq * 129:(hq + 1) * 129], lhsT=PT2[0][:, mc, h * 128:(h + 1) * 128],
                                          rhs=mvx[:, mc, h, :], start=(mc == 0), stop=(mc == 1)),
                                          reads=(PTB2[0][mc], mvxB), writes=(bankB,), signal=(hq == 1 and mc == 1))
                              bv = bank[:, 0:258].rearrange("p (h e) -> p h e", e=129)
                              DVE.op(lambda e: e.reciprocal(out=den[:, 0:2], in_=bv[:, :, 128]), reads=(bankB,), writes=(denB,))
                              DVE.op(lambda e: e.tensor_tensor(
                                  out=xo[:, half * 256:(half + 1) * 256].rearrange("p (h d) -> p h d", d=128), in0=bv[:, :, 0:128],
                                  in1=den[:, 0:2].unsqueeze(2).to_broadcast([128, 2, 128]), op=ALU.mult),
                                  reads=(bankB, denB), writes=(xoB,))
                          ACT.op(lambda e: e.activation(out=junk[:, 0:512], in_=xo[:], func=AF.Square, scale=float(512 ** -0.5), accum_out=stC[:, 2:3]),
                                 reads=(xoB,), writes=(stCB, junkB))

                      if part == "head":
                          if j + 1 < len(cset):
                              conv_shifts(cset[j + 1], (j + 1) % 2)

                          def consume_q(bank, bankB, i0, cnt):
                              ACT.op(lambda e: e.activation(out=qT[:, i0 * 128:(i0 + cnt) * 128], in_=bank[0:64, 0:cnt * 128],
                                                            func=AF.Copy), reads=(bankB,), writes=(qTB,))
                          transposes(lambda h: qr[r5][:, h * 64:(h + 1) * 64], 16, 64, qB[r5], consume_q)

                          def consume_mq(bank, bankB, i0, cnt):
                              ACT.op(lambda e: e.activation(out=mqT[:, :], in_=bank[:, 0:512], func=AF.Copy),
                                     reads=(bankB,), writes=(mqTB,))
                          transposes(lambda h: mqr[r5][:, h * 128:(h + 1) * 128], 4, 128, mqB[r5], consume_mq)
                          yield
                          s1(0)
                          yield
                          s1(1)
                          yield
                          return

                      if part == "mid":
                          s2(0)
                          conv()
                          yield
                          s1(2)
                          yield
                          s2(1)
                          yield
                          s1(3)
                          yield
                          s2(2)
                          yield
                          s2(3)
                          ACT.op(lambda e: e.activation(out=junk[:, 0:1024], in_=attn[:], func=AF.Square, scale=float(1024 ** -0.5),
                                                        accum_out=stC[:, 0:1]), reads=(attnB,), writes=(stCB, junkB))
                          yield
                          mem1()
                          yield
                          mem2()
                          ACT.op(lambda e: e.activation(out=junk[:, 0:512], in_=cva[:], func=AF.Square, scale=scr[r][:, 2:3],
                                                        accum_out=stC[:, 1:2]), reads=(cvaB, scB[r]), writes=(stCB, junkB))
                          yield
                          return

                      if part == "tailB":
                          def consume_m(bank, bankB, i0, cnt):
                              copy_alt(hT[:, i0:i0 + cnt, j * 128:(j + 1) * 128],
                                       bank[:, 0:cnt * 128].rearrange("p (c t) -> p c t", t=128), (bankB,), (hTB[j],))
                          transposes(lambda c: mixed[:, c * 128:(c + 1) * 128], 16, 128, mixedB, consume_m)
                          yield
                          return

                      assert part == "tailA"
                      DVE.op(lambda e: e.tensor_scalar(out=stC[:, 4:7], in0=stC[:, 0:3], scalar1=EPS, scalar2=None, op0=ALU.add),
                             reads=(stCB,), writes=(stCB,))
                      rsqrt_pool(stC[:, 4:7], stC[:, 8:11], 3, stCB, stCB)
                      GP.op(lambda e: e.tensor_tensor(out=stC[:, 16:19], in0=stC[:, 8:11], in1=scr[r][:, 3:6], op=ALU.mult),
                            reads=(stCB, scB[r]), writes=(stCB,))
                      DVE.op(lambda e: e.scalar_tensor_tensor(out=mixed[:, 0:1024], in0=attn[:], scalar=stC[:, 16:17],
                                                              in1=gsr[r5][:, 0:1024], op0=ALU.mult, op1=ALU.mult),
                             reads=(attnB, stCB, gsB[r5]), writes=(mixedB,))
                      DVE.op(lambda e: e.scalar_tensor_tensor(out=mixed[:, 1024:1536], in0=cva[:], scalar=stC[:, 17:18],
                                                              in1=gsr[r5][:, 1024:1536], op0=ALU.mult, op1=ALU.mult),
                             reads=(cvaB, stCB, gsB[r5]), writes=(mixedB,))
                      DVE.op(lambda e: e.scalar_tensor_tensor(out=mixed[:, 1536:2048], in0=xo[:], scalar=stC[:, 18:19],
                                                              in1=gsr[r5][:, 1536:2048], op0=ALU.mult, op1=ALU.mult),
                             reads=(xoB, stCB, gsB[r5]), writes=(mixedB,))
                      yield

                  def gen_D(half, h0):
                      for k2, n in enumerate(half):
                          src, srcB = src_rows(n)
                          SP.dma(xbuf[k2][:], src, ds_x[k2], reads=(() if srcB is None else (srcB,)), writes=(xB[k2],))
                      for oc in range(4):
                          wt, wtB = w_next("out", l, oc)
                          for k2, n in enumerate(half):
                              j = h0 + k2
                              k = mmi[0] % 2
                              mmi[0] += 1
                              for c in range(16):
                                  PE.op(lambda e: e.matmul(mm[k][:, :], lhsT=hT[:, c, j * 128:(j + 1) * 128],
                                                           rhs=wt[:, c, :], start=(c == 0), stop=(c == 15)),
                                        reads=(wtB, hTB[j]), writes=(mmB[k],), signal=(c == 15))
                              DVE.op(lambda e: e.tensor_tensor(
                                  out=xbuf[k2][:, oc * 512:(oc + 1) * 512], in0=mm[k][:, :], in1=xbuf[k2][:, oc * 512:(oc + 1) * 512],
                                  op=ALU.add), reads=(mmB[k], xB[k2]), writes=(xB[k2],))
                              if k2 == len(half) - 1:
                                  w_release()
                              yield
                      for k2, n in enumerate(half):
                          if not last:
                              SP.dma(xs[l][(n - l - 1) * 128:(n - l) * 128, :], xbuf[k2][:], ds_st[k2],
                                     reads=(xB[k2],), writes=(xsB[l][n],))
                          else:
                              ACT.op(lambda e: e.activation(out=junk[:], in_=xbuf[k2][:], func=AF.Square, scale=float(D ** -0.5),
                                                            accum_out=stA[:, 4:5]), reads=(xB[k2],), writes=(stAB, junkB))
                              DVE.op(lambda e: e.tensor_scalar(out=stA[:, 5:6], in0=stA[:, 4:5], scalar1=EPS, scalar2=None,
                                                               op0=ALU.add), reads=(stAB,), writes=(stAB,))
                              rsqrt_pool(stA[:, 5:6], stA[:, 6:7], 1, stAB, stAB)
                              DVE.op(lambda e: e.scalar_tensor_tensor(out=xbuf[k2][:], in0=xbuf[k2][:], scalar=stA[:, 6:7],
                                                                      in1=gfin[:], op0=ALU.mult, op1=ALU.mult),
                                     reads=(xB[k2], stAB, gfinB), writes=(xB[k2],))
                              SP.dma(yout[(n - NL) * 128:(n - NL + 1) * 128, :], xbuf[k2][:], ds_st[k2], reads=(xB[k2],), writes=())

                  def run(g):
                      for _ in g:
                          pass

                  def chain(*gs):
                      for g in gs:
                          yield from g

                  def interleave(main, fill, every):
                      cnt = 0
                      alive = True
                      for _ in main:
                          cnt += 1
                          if alive and cnt % every == 0:
                              try:
                                  next(fill)
                              except StopIteration:
                                  alive = False
                      if alive:
                          run(fill)

                  nC = len(cset)
                  G = lambda j, part: gen_C(cset[j], j, part)
                  run(G(0, "head"))
                  run(G(0, "mid"))
                  run(G(0, "tailA"))
                  run(G(1, "head"))
                  run(G(0, "tailB"))
                  run(G(1, "mid"))
                  run(G(1, "tailA"))
                  if nC == 4:
                      run(G(2, "head"))
                      run(G(1, "tailB"))
                      interleave(chain(G(2, "mid"), G(2, "tailA"), G(3, "head"), G(2, "tailB"), G(3, "mid"), G(3, "tailA"),
                                       G(3, "tailB")), gen_D(cset[0:2], 0), ILV)
                      chk(4)
                      if ti + 1 < len(tiles_l):
                          tbn = tiles_l[ti + 1][0]

                          def gen_Apre():
                              SP.dma(tab[0][:, 0:len(tbn), :], cst[:, tbn[0]:tbn[0] + len(tbn), :], ds_tab[0], writes=(tabB[0],))
                              for jn in range(min(2, len(tbn))):
                                  idxn = tbn[jn]
                                  srcn, srcBn = src_rows(idxn)
                                  SP.dma(xa[:], srcn, ds_xa, reads=(() if srcBn is None else (srcBn,)), writes=(xaB,))
                                  norm_block_to_hT(xa, xaB, gin[:], ginB, jn, False, scr[idxn % R], scB[idxn % R], part="pre")
                                  yield
                                  norm_T(jn)
                                  yield
                          interleave(gen_D(cset[2:4], 2), gen_Apre(), 1)
                          preA[ti + 1] = tuple(range(min(2, len(tbn))))
                      else:
                          run(gen_D(cset[2:4], 2))
                  else:
                      assert nC == 2
                      run(G(1, "tailB"))
                      chk(4)
                      run(gen_D(cset[0:2], 0))
        except _Stop:
            pass
        if stop is None:
            assert wstate["next_use"] == len(sched)
        final_waits = [(d.sem, d.cnt) for d in ds_st if d.cnt > 0]

        def replay(engine, key, extra_waits=()):
            for (name, a, kw, h) in streams[key]:
                inst = getattr(engine, name)(*a, **kw)
                if h["inc"] is not None:
                    inst.then_inc(*h["inc"])
            for (sm, v) in extra_waits:
                engine.wait_ge(sm, v)

        @blk.tensor
        def _(e):
            replay(e, "pe")

        @blk.scalar
        def _(e):
            replay(e, "act")

        @blk.vector
        def _(e):
            replay(e, "dve")

        @blk.gpsimd
        def _(e):
            replay(e, "gp")

        @blk.sync
        def _(e):
            replay(e, "sp", final_waits)

    return nc


def rope_table(pos):
    inv_freq = (np.float32(10000.0) ** (-(np.arange(0, 64, 2, dtype=np.float32)) / np.float32(64))).astype(np.float32)
    ang = (pos.astype(np.float32)[:, None] * inv_freq[None, :]).astype(np.float32)
    c = np.cos(ang).astype(np.float32)
    s = np.sin(ang).astype(np.float32)
    return np.concatenate([c, c, -s, s], axis=1)


def core_inputs(xseq, a, nown, mem, NB0, NL, shared):
    S = xseq.shape[0]
    halo = 128 * NL
    lo = a - halo
    n = NB0 * 128
    assert n == nown + 2 * halo
    slab = np.zeros((n, D), np.float32)
    s0, s1 = max(lo, 0), min(lo + n, S)
    slab[s0 - lo:s1 - lo] = xseq[s0:s1]
    pos = np.arange(lo, lo + n)
    valid = (pos >= 0) & (pos < S)
    kbv = np.where(valid, 0.0, NEG).astype(np.float32)
    tabv = rope_table(np.maximum(pos, 0))
    d = dict(shared)
    d["xin"] = slab
    d["memin"] = np.ascontiguousarray(mem, dtype=np.float32)
    d["cst"] = np.ascontiguousarray(tabv.reshape(NB0, 128, 128).transpose(1, 0, 2))
    d["kb"] = np.ascontiguousarray(kbv.reshape(NB0, 128).T)
    return d


def shared_inputs(norm_in, w_in, attn_sink, conv_w, norm_mem, w_mem_kv, g_attn, g_conv, g_mem, w_out, final_norm, NL):
    f = lambda a: np.ascontiguousarray(a, dtype=np.float32)
    eye = np.eye(128, dtype=np.float32)
    kk, qq = np.meshgrid(np.arange(128), np.arange(128), indexing="ij")
    cn = np.concatenate([eye, (kk >= qq).astype(np.float32), (kk <= qq).astype(np.float32)], axis=1)
    gcat = np.concatenate([g_attn, g_conv, g_mem], axis=1)[:NL]
    gcol = gcat.reshape(NL, 16, 128).transpose(2, 0, 1).reshape(128, NL * 16)
    bc = lambda v: np.broadcast_to(np.asarray(v, np.float32)[:, None, :], (v.shape[0], 128, v.shape[1]))
    return {
        "cnst": f(cn),
        "gin": f(bc(norm_in[:NL])),
        "gmem": f(bc(norm_mem[:NL])),
        "gfin": f(np.broadcast_to(np.asarray(final_norm, np.float32)[None, :], (128, D))),
        "cw": f(bc(np.asarray(conv_w, np.float32)[:NL].reshape(NL, 1536))),
        "sink": f(bc(attn_sink[:NL])),
        "gcol": f(gcol),
        "w_in": f(w_in[:NL]),
        "w_out": f(w_out[:NL]),
        "w_mkv": f(w_mem_kv[:NL]),
    }


_NC_CACHE = {}


def kernel(x_prompt, x_sample, mem_prompt, mem_sample, norm_in, w_in, attn_sink, conv_w,
           norm_mem, w_mem_kv, g_attn, g_conv, g_mem, w_out, final_norm):
    NL = 2
    NOWN = 4096
    NB0 = (NOWN + 2 * 128 * NL) // 128
    x_prompt = np.asarray(x_prompt, np.float32)
    x_sample = np.asarray(x_sample, np.float32)
    mem_prompt = np.asarray(mem_prompt, np.float32)
    mem_sample = np.asarray(mem_sample, np.float32)
    shared = shared_inputs(np.asarray(norm_in), np.asarray(w_in), np.asarray(attn_sink), np.asarray(conv_w),
                           np.asarray(norm_mem), np.asarray(w_mem_kv), np.asarray(g_attn), np.asarray(g_conv),
                           np.asarray(g_mem), np.asarray(w_out), np.asarray(final_norm), NL)
    assign = []
    for b in range(2):
        for h in range(2):
            assign.append((x_prompt[b], h * NOWN, mem_prompt[b]))
    for q in range(4):
        assign.append((x_sample[0], q * NOWN, mem_sample[0]))
    in_maps = [core_inputs(xs_, a, NOWN, m, NB0, NL, shared) for (xs_, a, m) in assign]
    key = (NB0, NL)
    if key not in _NC_CACHE:
        _NC_CACHE[key] = build_program(NB0, NL)
    nc = _NC_CACHE[key]
    res = run_bass_kernel_spmd(nc, in_maps, core_ids=list(range(8)))
    outs = [np.asarray(r["y"], np.float32) for r in res.results]
    y_prompt = np.stack([np.concatenate(outs[0:2], axis=0), np.concatenate(outs[2:4], axis=0)], axis=0)
    y_sample = np.concatenate(outs[4:8], axis=0)[None]
    return (y_prompt, y_sample)
```
